# Optimizing a Trainium2 kernel written in Bass

```python
import math
import jax, jax.numpy as jnp
from jax import lax
import numpy as np

D_MODEL = 2048
BATCH = 4
SEQ = 2048
DEPTH = 4
DEC_BATCH = 2
DEC_SEQ = 16384
PAST_LEN = 128

N_MIXERS = 2
N_ATTN_LAYERS = (DEPTH + 1) // 2
N_HYENA_LAYERS = DEPTH // 2
HEAD_DIM = 128
HEADS_PER_GROUP = D_MODEL // HEAD_DIM
GROUP_WINDOWS = (128, 512, 2048)
GROUP_DILATIONS = (1, 4, 16)
N_GROUPS = 3
QKV_WIDTH = N_GROUPS * 3 * HEADS_PER_GROUP * HEAD_DIM
ATTN_WIDTH = HEADS_PER_GROUP * HEAD_DIM
ROPE_THETA = 10000.0
D_FF = 4 * D_MODEL
SHORT_CONV = 3
FILTER_EMB = 33
FILTER_BANDS = (FILTER_EMB - 1) // 2
FILTER_HIDDEN = 64
DECAY_TARGET = 1e-2
FAST_DECAY_PCT = 0.3
SLOW_DECAY_PCT = 1.5
EPS = 1e-6

kernel_name = "hybrid_dilated_attn_hyena_encoder"


def rms_norm(x, g):
    xf = x.astype(jnp.float32)
    y = xf * lax.rsqrt(jnp.mean(xf * xf, axis=-1, keepdims=True) + EPS)
    return (y * g.astype(jnp.float32)).astype(x.dtype)


def rope_tables(S):
    inv = ROPE_THETA ** (-jnp.arange(0, HEAD_DIM, 2, dtype=jnp.float32) / HEAD_DIM)
    ang = jnp.arange(S, dtype=jnp.float32)[:, None] * inv[None, :]
    return jnp.cos(ang), jnp.sin(ang)


def apply_rope(t, cos, sin):
    half = HEAD_DIM // 2
    t1, t2 = t[..., :half], t[..., half:]
    c = cos[None, :, None, :]
    s = sin[None, :, None, :]
    return jnp.concatenate([t1 * c - t2 * s, t2 * c + t1 * s], axis=-1)


def dilated_window_attention(q, k, v, dilation, half):
    B, S, H, Dh = q.shape
    n = S // dilation
    nb = -(-n // half)
    L = nb * half

    def by_stride(t):
        return t.reshape(B, n, dilation, H, Dh).transpose(0, 2, 3, 1, 4)

    qb = jnp.pad(by_stride(q), ((0, 0), (0, 0), (0, 0), (0, L - n), (0, 0))).reshape(B, dilation, H, nb, half, Dh)

    def neighbours(t):
        tp = jnp.pad(by_stride(t), ((0, 0), (0, 0), (0, 0), (half, L - n + half), (0, 0))).reshape(B, dilation, H, nb + 2, half, Dh)
        return jnp.concatenate([tp[:, :, :, :-2], tp[:, :, :, 1:-1], tp[:, :, :, 2:]], axis=4)

    kb = neighbours(k)
    vb = neighbours(v)
    s = jnp.einsum('brhnqd,brhnkd->brhnqk', qb, kb)
    jq = jnp.arange(nb)[:, None] * half + jnp.arange(half)[None, :]
    jk = (jnp.arange(nb)[:, None] - 1) * half + jnp.arange(3 * half)[None, :]
    rel = jk[:, None, :] - jq[:, :, None]
    valid = (jnp.abs(rel) <= half) & (jk[:, None, :] >= 0) & (jk[:, None, :] < n)
    s = jnp.where(valid, s, -jnp.inf)
    m = jnp.max(s, axis=-1, keepdims=True)
    p = jnp.exp(s - m)
    den = jnp.sum(p, axis=-1)
    o = jnp.einsum('brhnqk,brhnkd->brhnqd', p, vb) / den[..., None]
    lse = m[..., 0] + jnp.log(den)
    o = o.reshape(B, dilation, H, L, Dh)[:, :, :, :n].transpose(0, 3, 1, 2, 4).reshape(B, S, H, Dh)
    lse = lse.reshape(B, dilation, H, L)[..., :n].transpose(0, 3, 1, 2).reshape(B, S, H)
    return o, lse


def attention_mixer(xn, w_qkv, q_gain, k_gain, w_out, cos, sin):
    B, S, _ = xn.shape
    qkv = (xn @ w_qkv).reshape(B, S, N_GROUPS, 3, HEADS_PER_GROUP, HEAD_DIM).astype(jnp.float32)
    scale = HEAD_DIM ** -0.5
    outs = []
    lses = []
    for g in range(N_GROUPS):
        q = apply_rope(rms_norm(qkv[:, :, g, 0], q_gain[g]), cos, sin) * scale
        k = apply_rope(rms_norm(qkv[:, :, g, 1], k_gain[g]), cos, sin)
        dil = GROUP_DILATIONS[g]
        o, l = dilated_window_attention(q, k, qkv[:, :, g, 2], dil, GROUP_WINDOWS[g] // (2 * dil))
        outs.append(o)
        lses.append(l)
    w = jax.nn.softmax(jnp.stack(lses), axis=0)
    o = jnp.einsum('gbsh,gbshd->bshd', w, jnp.stack(outs))
    return o.reshape(B, S, ATTN_WIDTH).astype(xn.dtype) @ w_out


def implicit_filters(L, w1, b1, w2, b2, w3, b3, freq, w4):
    f32 = jnp.float32
    pos = jnp.arange(L, dtype=f32)
    t = pos / (L - 1)
    bands = jnp.linspace(1e-4, FILTER_BANDS - 1, FILTER_BANDS, dtype=f32)
    ang = (2.0 * math.pi / L) * pos[:, None] * bands[None, :]
    z = jnp.concatenate([t[:, None], jnp.cos(ang), -jnp.sin(ang)], axis=-1)
    fr = freq.astype(f32)
    h = jnp.sin(fr * (z @ w1.astype(f32) + b1.astype(f32)))
    h = jnp.sin(fr * (h @ w2.astype(f32) + b2.astype(f32)))
    h = jnp.sin(fr * (h @ w3.astype(f32) + b3.astype(f32)))
    h = h @ w4.astype(f32)
    deltas = jnp.abs(jnp.linspace(math.log(DECAY_TARGET) / SLOW_DECAY_PCT, math.log(DECAY_TARGET) / FAST_DECAY_PCT, D_MODEL, dtype=f32))
    decay = jnp.exp(-t[:, None] * deltas[None, :])
    return h[:, :D_MODEL] * decay, h[:, D_MODEL:] * decay


def bidirectional_long_conv(z, h_fwd, h_bwd):
    L = z.shape[1]
    kern = jnp.concatenate([h_fwd, jnp.zeros((1, D_MODEL), jnp.float32), h_bwd[1:][::-1]], axis=0)
    zf = jnp.fft.rfft(z, n=2 * L, axis=1)
    kf = jnp.fft.rfft(kern, axis=0)
    return jnp.fft.irfft(zf * kf[None], n=2 * L, axis=1)[:, :L]


def hyena_mixer(xn, w_in, b_in, conv_w, conv_b, fw1, fb1, fw2, fb2, fw3, fb3, ffreq, fw4, skip, w_out, b_out):
    B, L, _ = xn.shape
    u = xn @ w_in + b_in
    up = jnp.pad(u, ((0, 0), (1, 1), (0, 0)))
    u = up[:, :-2] * conv_w[0] + up[:, 1:-1] * conv_w[1] + up[:, 2:] * conv_w[2] + conv_b
    x0, x1, v = jnp.split(u.astype(jnp.float32), 3, axis=-1)
    h_fwd, h_bwd = implicit_filters(L, fw1, fb1, fw2, fb2, fw3, fb3, ffreq, fw4)
    z = v * x1
    z = bidirectional_long_conv(z, h_fwd, h_bwd) + z * skip.astype(jnp.float32)
    y = (z * x0).astype(xn.dtype)
    return y @ w_out + b_out


def squared_relu_mlp(xn, w1, w2):
    h = jax.nn.relu(xn @ w1)
    return (h * h) @ w2


def trunk(x, mix_norm, mlp_norm, attn_w_qkv, attn_q_gain, attn_k_gain, attn_w_out,
          hy_w_in, hy_b_in, hy_conv_w, hy_conv_b, hy_filt_w1, hy_filt_b1, hy_filt_w2, hy_filt_b2,
          hy_filt_w3, hy_filt_b3, hy_filt_freq, hy_filt_w4, hy_skip, hy_w_out, hy_b_out, mlp_w1, mlp_w2):
    cos, sin = rope_tables(x.shape[1])
    for i in range(DEPTH):
        xn = rms_norm(x, mix_norm[i])
        j = i // N_MIXERS
        if i % N_MIXERS == 0:
            x = x + attention_mixer(xn, attn_w_qkv[j], attn_q_gain[j], attn_k_gain[j], attn_w_out[j], cos, sin)
        else:
            x = x + hyena_mixer(xn, hy_w_in[j], hy_b_in[j], hy_conv_w[j], hy_conv_b[j],
                                hy_filt_w1[j], hy_filt_b1[j], hy_filt_w2[j], hy_filt_b2[j],
                                hy_filt_w3[j], hy_filt_b3[j], hy_filt_freq[j], hy_filt_w4[j],
                                hy_skip[j], hy_w_out[j], hy_b_out[j])
        x = x + squared_relu_mlp(rms_norm(x, mlp_norm[i]), mlp_w1[i], mlp_w2[i])
    return x


def setup_inputs(seed: int = 0) -> dict:
    key = jax.random.key(seed)
    ks = jax.random.split(key, 26)

    def nrm(k, shape, scale):
        return jax.random.normal(k, shape, jnp.float32) * scale

    D = D_MODEL
    NA = N_ATTN_LAYERS
    NH = N_HYENA_LAYERS
    FH = FILTER_HIDDEN
    return {
        "x_prompt": nrm(ks[0], (BATCH, SEQ, D), 1.0),
        "x_sample": nrm(ks[1], (DEC_BATCH, DEC_SEQ, D), 1.0),
        "mix_norm": 1.0 + nrm(ks[2], (DEPTH, D), 0.02),
        "mlp_norm": 1.0 + nrm(ks[3], (DEPTH, D), 0.02),
        "attn_w_qkv": nrm(ks[4], (NA, D, QKV_WIDTH), D ** -0.5),
        "attn_q_gain": 1.0 + nrm(ks[5], (NA, N_GROUPS, HEAD_DIM), 0.02),
        "attn_k_gain": 1.0 + nrm(ks[6], (NA, N_GROUPS, HEAD_DIM), 0.02),
        "attn_w_out": nrm(ks[7], (NA, ATTN_WIDTH, D), ATTN_WIDTH ** -0.5),
        "hy_w_in": nrm(ks[8], (NH, D, 3 * D), D ** -0.5),
        "hy_b_in": nrm(ks[9], (NH, 3 * D), 0.02),
        "hy_conv_w": nrm(ks[10], (NH, SHORT_CONV, 3 * D), 0.5),
        "hy_conv_b": nrm(ks[11], (NH, 3 * D), 0.02),
        "hy_filt_w1": nrm(ks[12], (NH, FILTER_EMB, FH), FILTER_EMB ** -0.5),
        "hy_filt_b1": nrm(ks[13], (NH, FH), 0.02),
        "hy_filt_w2": nrm(ks[14], (NH, FH, FH), FH ** -0.5),
        "hy_filt_b2": nrm(ks[15], (NH, FH), 0.02),
        "hy_filt_w3": nrm(ks[16], (NH, FH, FH), FH ** -0.5),
        "hy_filt_b3": nrm(ks[17], (NH, FH), 0.02),
        "hy_filt_freq": 1.0 + nrm(ks[18], (NH, FH), 0.02),
        "hy_filt_w4": nrm(ks[19], (NH, FH, 2 * D), 0.03 * FH ** -0.5),
        "hy_skip": nrm(ks[20], (NH, D), 0.5),
        "hy_w_out": nrm(ks[21], (NH, D, D), D ** -0.5),
        "hy_b_out": nrm(ks[22], (NH, D), 0.02),
        "mlp_w1": nrm(ks[23], (DEPTH, D, D_FF), D ** -0.5),
        "mlp_w2": nrm(ks[24], (DEPTH, D_FF, D), D_FF ** -0.5),
    }


def reference(x_prompt, x_sample, mix_norm, mlp_norm, attn_w_qkv, attn_q_gain, attn_k_gain, attn_w_out,
              hy_w_in, hy_b_in, hy_conv_w, hy_conv_b, hy_filt_w1, hy_filt_b1, hy_filt_w2, hy_filt_b2,
              hy_filt_w3, hy_filt_b3, hy_filt_freq, hy_filt_w4, hy_skip, hy_w_out, hy_b_out, mlp_w1, mlp_w2):
    weights = (mix_norm, mlp_norm, attn_w_qkv, attn_q_gain, attn_k_gain, attn_w_out,
               hy_w_in, hy_b_in, hy_conv_w, hy_conv_b, hy_filt_w1, hy_filt_b1, hy_filt_w2, hy_filt_b2,
               hy_filt_w3, hy_filt_b3, hy_filt_freq, hy_filt_w4, hy_skip, hy_w_out, hy_b_out, mlp_w1, mlp_w2)
    y_prompt = trunk(x_prompt, *weights)
    y_sample = trunk(x_sample, *weights)
    return (y_prompt, y_sample)
```

```python
import numpy as np
from contextlib import ExitStack
import concourse.bass as bass
import concourse.mybir as mybir
from concourse.bass_utils import run_bass_kernel_spmd

F32 = mybir.dt.float32
BF16 = mybir.dt.bfloat16
AF = mybir.ActivationFunctionType
ALU = mybir.AluOpType

D = 2048
KC = D // 128
DFF = 4 * D
T = 512
EPS = 1e-6
HD = 128
NH = 16
NG = 3
DILS = (1, 4, 16)
QKVW = NG * 3 * NH * HD
PAD = 1024
SEG = 2048
TB = 2048
WBLK = 8192
FT = 512
TWO_PI = 2.0 * np.pi


class Sem:
    def __init__(self, b, name):
        self.h = b.stack.enter_context(b.nc.semaphore(name))
        self.n = 0
        b.sems.append(self)

    def inc(self, ins, k=1):
        ins.then_inc(self.h, k)
        self.n += k
        return self.n

    def dma(self, ins):
        return self.inc(ins, 16)


class Builder:
    def __init__(self, ntok):
        self.ntok = ntok
        self.nc = bass.Bass("TRN2", target_bir_lowering=False)
        self.stack = ExitStack()
        self.sems = []
        self.dram = {}

    def din(self, name, shape, dt=F32):
        t = self.nc.dram_tensor(name, list(shape), dt, kind="ExternalInput").ap()
        self.dram[name] = t
        return t

    def dout(self, name, shape, dt=F32):
        t = self.nc.dram_tensor(name, list(shape), dt, kind="ExternalOutput").ap()
        self.dram[name] = t
        return t

    def dint(self, name, shape, dt):
        t = self.nc.dram_tensor(name, list(shape), dt).ap()
        self.dram[name] = t
        return t

    def sb(self, name, shape, dt):
        return self.stack.enter_context(self.nc.sbuf_tensor("sb_" + name, list(shape), dt))

    def ps(self, name, shape, dt=F32):
        return self.stack.enter_context(self.nc.psum_tensor(name, list(shape), dt))

    def sem(self, name):
        return Sem(self, name)

    def end_iter(self):
        nc = self.nc
        nc.all_engine_barrier()
        for s in self.sems:
            if s.n:
                nc.gpsimd.sem_clear(s.h)
                s.n = 0
        nc.all_engine_barrier()


from contextlib import contextmanager


@contextmanager
def fori(b, end):
    nc = b.nc
    if getattr(b, "loopregs", None) is None:
        b.loopregs = nc.alloc_registers("lp_shared", engines=mybir.ALL_ENGINES)
    regs = b.loopregs
    lid = nc.next_id()
    ls = f"myfori_{lid}_loop"
    le = f"myfori_{lid}_end"
    nc.regs_mov(regs, 0)
    nc.br(ls, engines=mybir.ALL_ENGINES)
    with nc.body(ls, valid_engines=mybir.ALL_ENGINES):
        yield nc.snap(regs, min_val=0, max_val=end - 1)
        nc.regs_alu(regs, regs, 1, op=mybir.AluOpType.add)
        nc.br_lt(regs, end, on_true=ls, on_false=le, engines=mybir.ALL_ENGINES)
    nc.switch_bb(le)


class Tracker:
    def __init__(self, b, ndma=8):
        nc = b.nc
        self.b = b
        self.eng = {"pe": nc.tensor, "act": nc.scalar, "dve": nc.vector, "pool": nc.gpsimd, "sp": nc.sync}
        self.esem = {e: b.sem("t_" + e) for e in ("pe", "act", "dve", "pool")}
        self.slots = [b.sem(f"t_dma{i}") for i in range(ndma)]
        self.reset()

    def reset(self):
        self.waited = {e: {} for e in self.eng}
        self.lastw = {}
        self.readers = {}
        self.nslot = 0

    def _wait(self, e, dep):
        sem, cnt = dep
        if self.waited[e].get(sem, 0) >= cnt:
            return
        self.eng[e].wait_ge(sem.h, cnt)
        self.waited[e][sem] = cnt

    def _deps(self, e, rd, wr):
        for k in rd:
            if k in self.lastw:
                self._wait(e, self.lastw[k])
        for k in wr:
            if k in self.lastw:
                self._wait(e, self.lastw[k])
            for r in self.readers.get(k, ()):
                self._wait(e, r)

    def _reg(self, dep, rd, wr):
        for k in rd:
            self.readers.setdefault(k, []).append(dep)
        for k in wr:
            self.lastw[k] = dep
            self.readers[k] = []

    def op(self, e, fn, rd=(), wr=()):
        self._deps(e, rd, wr)
        ins = fn(self.eng[e])
        cnt = self.esem[e].inc(ins)
        self._reg((self.esem[e], cnt), rd, wr)
        return ins

    def mm(self, out, ops, rd=(), wr=(), **kw):
        self._deps("pe", rd, wr)
        n = len(ops)
        for i, (l, r) in enumerate(ops):
            ins = self.b.nc.tensor.matmul(out, lhsT=l, rhs=r, start=(i == 0), stop=(i == n - 1), **kw)
        cnt = self.esem["pe"].inc(ins)
        self._reg((self.esem["pe"], cnt), rd, wr)

    def dma(self, out, in_, rd=(), wr=(), q="sp", **kw):
        slot = self.slots[self.nslot % len(self.slots)]
        self.nslot += 1
        if slot.n:
            self._wait(q, (slot, slot.n))
        self._deps(q, rd, wr)
        ins = self.eng[q].dma_start(out=out, in_=in_, **kw)
        cnt = slot.dma(ins)
        self._reg((slot, cnt), rd, wr)

    def maybe_reset(self, lim=600):
        if max(s.n for s in list(self.esem.values()) + self.slots) > lim:
            self.end_iter()
            return True
        return False

    def end_iter(self):
        for k, dep in list(self.lastw.items()):
            self._wait("sp", dep)
        for k, rs in list(self.readers.items()):
            for r in rs:
                self._wait("sp", r)
        self.b.end_iter()
        self.reset()


def wblocks(K, N):
    nk = K // 128
    cb = WBLK // nk
    return nk, cb, N // cb


def cast_blocked(b, tr, dst, src, K, N):
    nk, cb, nblk = wblocks(K, N)
    for j in range(nblk):
        tr.maybe_reset(300)
        i = b.cast_ctr % 2
        b.cast_ctr += 1
        sv = src[:, j * cb:(j + 1) * cb].rearrange("(kc p) n -> p kc n", p=128)
        fv = b.cf[i][:, :].rearrange("p (kc n) -> p kc n", kc=nk)
        keys = []
        for k0 in range(0, nk, 16):
            tr.dma(fv[:, k0:k0 + 16, :], sv[:, k0:k0 + 16, :], wr=[("cf", i, k0)])
            keys.append(("cf", i, k0))
        e = ("dve", "act", "pool")[b.cast_ctr % 3]
        if e == "act":
            tr.op(e, lambda en: en.activation(out=b.cbf[i][:, :], in_=b.cf[i][:, :], func=AF.Copy), rd=keys, wr=[("cbf", i)])
        else:
            tr.op(e, lambda en: en.tensor_copy(out=b.cbf[i][:, :], in_=b.cf[i][:, :]), rd=keys, wr=[("cbf", i)])
        tr.dma(dst[j], b.cbf[i][:, :], rd=[("cbf", i)])


def linear_tile(b, tr, rhs_of, rd_rhs, Wb, K, N, epi):
    nk, cb, nblk = wblocks(K, N)
    g = 0
    issued = 0
    for j in range(nblk):
        if tr.maybe_reset():
            issued = j
        while issued < min(nblk, j + 2):
            tr.dma(b.wbuf[issued % 3][:, :], Wb[issued], wr=[("w", issued % 3)])
            issued += 1
        wb = j % 3
        wv = b.wbuf[wb][:, :].rearrange("p (c n) -> p c n", c=nk)
        for nl in range(cb // 128):
            n = j * (cb // 128) + nl
            bk = ("bank", g % 4)
            bank = b.pbank[g % 4]
            tr.mm(bank[:, :], [(wv[:, kc, nl * 128:(nl + 1) * 128], rhs_of(kc)) for kc in range(nk)],
                  rd=[("w", wb)] + list(rd_rhs), wr=[bk])
            epi(n, bank, bk)
            g += 1


def norm_tile(b, tr, gvec):
    for c in range(KC):
        q = b.sq[c % 2]
        tr.op("act", lambda e: e.activation(out=q[:, :], in_=b.x_sb[:, c, :], func=AF.Square), rd=["x"], wr=[("sq", c % 2)])
        tr._deps("pe", [("sq", c % 2)], ["ps_ss"] if c == 0 else [])
        ins = b.nc.tensor.matmul(b.ps_ss[:, :], lhsT=b.ones_f[:, :], rhs=q[:, :], start=(c == 0), stop=(c == KC - 1))
        cnt = tr.esem["pe"].inc(ins)
        tr._reg((tr.esem["pe"], cnt), [("sq", c % 2)], ["ps_ss"])
    tr.op("act", lambda e: e.activation(out=b.rt[:, :], in_=b.ps_ss[:, :], func=AF.Sqrt, bias=b.eps_c[:, 0:1], scale=1.0 / D), rd=["ps_ss"], wr=["rt"])
    tr.op("dve", lambda e: e.reciprocal(out=b.rstd[:, :], in_=b.rt[:, :]), rd=["rt"], wr=["rstd"])
    for c in range(KC):
        tr.op("dve", lambda e: e.scalar_tensor_tensor(out=b.xn[:, c, :], in0=b.x_sb[:, c, :], scalar=gvec[:, c:c + 1],
                                                      in1=b.rstd[:, :], op0=ALU.mult, op1=ALU.mult), rd=["x", "rstd"], wr=["xn"])


def token_loop(b, tr, xsrc, pre, mlp, post):
    nc = b.nc
    XTv = b.XT.rearrange("(c p) t -> p c t", p=128)
    xsv = xsrc.rearrange("(c p) t -> p c t", p=128)
    d = b.dram
    h_sb = b.h_sb
    with fori(b, b.ntok // T) as it:
        tok = it * T
        tokp = it * T + PAD
        tr.dma(b.x_sb[:, :, :], xsv[:, :, bass.ds(tok, T)], wr=["x"])
        if pre is not None:
            Wb, bias = pre
            for c in range(KC):
                tr.dma(b.xn[:, c, :], d[f"OT{c}"][:, bass.ds(tok, T)], wr=["xn"])

            def epi0(n, bank, bk):
                if bias is None:
                    tr.op("dve", lambda e: e.tensor_tensor(out=b.x_sb[:, n, :], in0=b.x_sb[:, n, :], in1=bank[:, :], op=ALU.add), rd=[bk, "x"], wr=["x"])
                else:
                    tr.op("dve", lambda e: e.scalar_tensor_tensor(out=b.x_sb[:, n, :], in0=bank[:, :], scalar=bias[:, n:n + 1], in1=b.x_sb[:, n, :],
                                                                  op0=ALU.add, op1=ALU.add), rd=[bk, "x"], wr=["x"])
            linear_tile(b, tr, lambda kc: b.xn[:, kc, :], ["xn"], Wb, D, D, epi0)
        if mlp is not None:
            w1b, w2b, gvec = mlp
            norm_tile(b, tr, gvec)

            def epi1(n, bank, bk):
                s = n % 2
                rb = b.rbuf[s]
                tr.op("act", lambda e: e.activation(out=rb[:, :], in_=bank[:, :], func=AF.Relu), rd=[bk], wr=[("rb", s)])
                tr.op("dve", lambda e: e.tensor_tensor(out=h_sb[:, n, :], in0=rb[:, :], in1=rb[:, :], op=ALU.mult), rd=[("rb", s)], wr=["h"])
            linear_tile(b, tr, lambda kc: b.xn[:, kc, :], ["xn"], w1b, D, DFF, epi1)

            def epi2(n, bank, bk):
                tr.op("dve", lambda e: e.tensor_tensor(out=b.x_sb[:, n, :], in0=b.x_sb[:, n, :], in1=bank[:, :], op=ALU.add), rd=[bk, "x"], wr=["x"])
            linear_tile(b, tr, lambda kc: h_sb[:, kc, :], ["h"], w2b, DFF, D, epi2)
        if pre is not None or mlp is not None or xsrc is not b.XT:
            tr.dma(XTv[:, :, bass.ds(tok, T)], b.x_sb[:, :, :], rd=["x"])
        if post is not None and post[0] == "qkv":
            _, Wb, gvec, la = post
            norm_tile(b, tr, gvec)
            tr.dma(b.ropec[:, :], d["ropeC"][:, bass.ds(tok, T)], wr=["ropec"])
            tr.dma(b.ropes[:, :], d["ropeS"][:, bass.ds(tok, T)], wr=["ropes"])

            def epiq(n, bank, bk):
                g3, rem = divmod(n, 3 * NH)
                a, h = divmod(rem, NH)
                s = n % 2
                st = b.stage[s]
                skey = ("stg", s)
                import os as _os
                if a == 2 or _os.environ.get("QKVSIMPLE"):
                    tr.op("act", lambda e: e.activation(out=st[:, :], in_=bank[:, :], func=AF.Copy), rd=[bk], wr=[skey])
                else:
                    gcol = b.qkgain[:, la, g3 * 2 + a:g3 * 2 + a + 1]
                    sq = b.sq[s]
                    qg = b.qg[s]
                    tr.op("act", lambda e: e.activation(out=sq[:, :], in_=bank[:, :], func=AF.Square), rd=[bk], wr=[("sq", s)])
                    tr.op("act", lambda e: e.activation(out=qg[:, :], in_=bank[:, :], func=AF.Copy, scale=gcol), rd=[bk], wr=[("qg", s)])
                    tr.mm(b.ps_ss[:, :], [(b.ones_f[:, :], sq[:, :])], rd=[("sq", s)], wr=["ps_ss"])
                    tr.mm(b.ps_rq[:, :], [(b.perm_f[:, :], qg[:, :])], rd=[("qg", s)], wr=["ps_rq"])
                    tr.op("act", lambda e: e.activation(out=b.rt[:, :], in_=b.ps_ss[:, :], func=AF.Ln, bias=b.eps_c[:, 0:1], scale=1.0 / HD), rd=["ps_ss"], wr=["rt"])
                    tr.op("act", lambda e: e.activation(out=b.rstd[:, :], in_=b.rt[:, :], func=AF.Exp, scale=-0.5), rd=["rt"], wr=["rstd"])
                    tr.op("dve", lambda e: e.tensor_tensor(out=b.t1[:, :], in0=qg[:, :], in1=b.ropec[:, :], op=ALU.mult), rd=[("qg", s), "ropec"], wr=["t1"])
                    tr.op("dve", lambda e: e.tensor_tensor(out=b.t2[:, :], in0=b.ps_rq[:, :], in1=b.ropes[:, :], op=ALU.mult), rd=["ps_rq", "ropes"], wr=["t2"])
                    tr.op(_os.environ.get("ADDENG", "pool"), lambda e: e.tensor_tensor(out=b.t1[:, :], in0=b.t1[:, :], in1=b.t2[:, :], op=ALU.add), rd=["t1", "t2"], wr=["t1"])
                    tr.op("dve", lambda e: e.tensor_tensor(out=st[:, :], in0=b.t1[:, :], in1=b.rstd[:, :], op=ALU.mult), rd=["t1", "rstd"], wr=[skey])
                tr.dma(d[f"QKV{n}"][:, bass.ds(tokp, T)], st[:, :], rd=[skey])
            linear_tile(b, tr, lambda kc: b.xn[:, kc, :], ["xn"], Wb, D, QKVW, epiq)
        if post is not None and post[0] == "hyin":
            _, Wb, gvec, lh = post
            norm_tile(b, tr, gvec)

            def epih(n, bank, bk):
                s = n % 2
                tr.op("act", lambda e: e.activation(out=b.rbuf[s][:, :], in_=bank[:, :], func=AF.Identity, bias=b.hyb[:, lh, 0, n:n + 1], scale=1.0), rd=[bk], wr=[("rb", s)])
                tr.dma(d[f"UT{n}"][:, bass.ds(tok, T)], b.rbuf[s][:, :], rd=[("rb", s)])
            linear_tile(b, tr, lambda kc: b.xn[:, kc, :], ["xn"], Wb, D, 3 * D, epih)
        tr.end_iter()


def attn_core(b, tr, nseg):
    nc = b.nc
    d = b.dram

    def loads(h, sg):
        p = h % 2
        for g3 in range(NG):
            nq = (g3 * 3 + 0) * NH + h
            nk_ = (g3 * 3 + 1) * NH + h
            nv = (g3 * 3 + 2) * NH + h
            tr.dma(b.qT[p][g3][:, :], d[f"QKV{nq}"][:, bass.ds(sg * SEG + PAD, SEG)], wr=[("qT", p, g3)])
            tr.dma(b.kT[p][g3][:, :], d[f"QKV{nk_}"][:, bass.ds(sg * SEG, SEG + 2 * PAD)], wr=[("kT", p, g3)])
            tr.dma(b.vT[p][g3][:, :], d[f"QKV{nv}"][:, bass.ds(sg * SEG, SEG + 2 * PAD)], wr=[("vT", p, g3)])

    with fori(b, nseg) as sg:
        tr.dma(b.smask[:, :], d["MASKTB"][bass.ds(sg, 1), :, :].rearrange("o p m -> (o p) m"), wr=["smask"])
        nacc = 0
        for h in range(NH):
            if h > 0:
                tr.end_iter()
                tr.dma(b.smask[:, :], d["MASKTB"][bass.ds(sg, 1), :, :].rearrange("o p m -> (o p) m"), wr=["smask"])
            loads(h, sg)
            p = h % 2
            anum, aden, obf = b.anum[p], b.aden[p], b.obf[p]
            ka, kd_, ko = ("anum", p), ("aden", p), ("obf", p)
            for g3 in range(NG):
                dd = DILS[g3]
                n = SEG // dd
                nkb = n // 128 + 1
                qT, kT, vT = b.qT[p][g3], b.kT[p][g3], b.vT[p][g3]
                kq, kk, kv = ("qT", p, g3), ("kT", p, g3), ("vT", p, g3)
                for r in range(dd):
                    for m in range(nkb):
                        qa = max(0, 128 * (m - 1))
                        qb_ = min(n, 128 * (m + 1))
                        nq = qb_ - qa
                        kcol = PAD + r + dd * (128 * m - 64)
                        ksl = slice(kcol, kcol + 127 * dd + 1, dd)
                        qcol = r + dd * qa
                        qsl = slice(qcol, qcol + (nq - 1) * dd + 1, dd)
                        if m == 0:
                            mview = b.smask[:, 0:128]
                            mk = ["smask"]
                        elif m == nkb - 1:
                            mview = b.smask[:, 128:256]
                            mk = ["smask"]
                        else:
                            mview = b.mask[:, 0:2, :].rearrange("p a b -> p (a b)")
                            mk = []
                        i2 = nacc % 2
                        nacc += 1
                        sT = b.ps_s[i2]
                        tr.mm(sT[:, 0:nq], [(kT[:, ksl], qT[:, qsl])], rd=[kk, kq], wr=[("ps_s", i2)])
                        tr.op("pe", lambda e: e.transpose(out=b.ps_vt[i2][:, 0:128], in_=vT[:, ksl], identity=b.ident_b[:, :]), rd=[kv], wr=[("ps_vt", i2)])
                        tr.op("act", lambda e: e.activation(out=b.pexp[i2][:, 0:nq], in_=sT[:, 0:nq], func=AF.Exp), rd=[("ps_s", i2)], wr=[("pexp", i2)])
                        tr.op("act", lambda e: e.activation(out=b.vblk[i2][:, :], in_=b.ps_vt[i2][:, 0:128], func=AF.Copy), rd=[("ps_vt", i2)], wr=[("vblk", i2)])
                        tr.op("pool", lambda e: e.tensor_tensor(out=b.pm[i2][:, 0:nq], in0=b.pexp[i2][:, 0:nq], in1=mview, op=ALU.mult), rd=[("pexp", i2)] + mk, wr=[("pm", i2)])
                        tr.mm(b.ps_num[i2][:, 0:nq], [(b.vblk[i2][:, :], b.pm[i2][:, 0:nq])], rd=[("vblk", i2), ("pm", i2)], wr=[("ps_num", i2)])
                        tr.mm(b.ps_den[i2][:, 0:nq], [(b.ones_b[:, :], b.pm[i2][:, 0:nq])], rd=[("pm", i2)], wr=[("ps_den", i2)])
                        asl = slice(qcol, qcol + (nq - 1) * dd + 1, dd)
                        pn, pd_ = b.ps_num[i2], b.ps_den[i2]
                        kn, kd = ("ps_num", i2), ("ps_den", i2)
                        if g3 == 0 and m == 0:
                            tr.op("dve", lambda e: e.tensor_copy(out=anum[:, asl], in_=pn[:, 0:nq]), rd=[kn], wr=[ka])
                            tr.op("dve", lambda e: e.tensor_copy(out=aden[:, asl], in_=pd_[:, 0:nq]), rd=[kd], wr=[kd_])
                        elif g3 == 0 and nq == 256:
                            a1 = slice(qcol, qcol + 128)
                            a2 = slice(qcol + 128, qcol + 256)
                            tr.op("dve", lambda e: e.tensor_tensor(out=anum[:, a1], in0=anum[:, a1], in1=pn[:, 0:128], op=ALU.add), rd=[kn, ka], wr=[ka])
                            tr.op("dve", lambda e: e.tensor_copy(out=anum[:, a2], in_=pn[:, 128:256]), rd=[kn], wr=[ka])
                            tr.op("dve", lambda e: e.tensor_tensor(out=aden[:, a1], in0=aden[:, a1], in1=pd_[:, 0:128], op=ALU.add), rd=[kd, kd_], wr=[kd_])
                            tr.op("dve", lambda e: e.tensor_copy(out=aden[:, a2], in_=pd_[:, 128:256]), rd=[kd], wr=[kd_])
                        else:
                            tr.op("dve", lambda e: e.tensor_tensor(out=anum[:, asl], in0=anum[:, asl], in1=pn[:, 0:nq], op=ALU.add), rd=[kn, ka], wr=[ka])
                            tr.op("dve", lambda e: e.tensor_tensor(out=aden[:, asl], in0=aden[:, asl], in1=pd_[:, 0:nq], op=ALU.add), rd=[kd, kd_], wr=[kd_])
            tr.op("dve", lambda e: e.reciprocal(out=aden[:, :], in_=aden[:, :]), rd=[kd_], wr=[kd_])
            tr.op("dve", lambda e: e.tensor_tensor(out=obf[:, :], in0=anum[:, :], in1=aden[:, :], op=ALU.mult), rd=[ka, kd_], wr=[ko])
            tr.dma(d[f"OT{h}"][:, bass.ds(sg * SEG, SEG)], obf[:, :], rd=[ko])
        tr.end_iter()


def hy_conv_gate(b, tr, seqs, lh):
    d = b.dram
    starts = {s0 for (s0, L) in seqs}
    ends = {s0 + L for (s0, L) in seqs}
    for cc in range(KC):
        for t0 in range(0, b.ntok, TB):
            for part in range(3):
                ub = b.ubuf[part]
                UT = d[f"UT{part * KC + cc}"]
                lo = t0 - 1 if t0 not in starts else t0
                hi = t0 + TB + 1 if (t0 + TB) not in ends else t0 + TB
                if t0 in starts:
                    tr.op("dve", lambda e: e.memset(ub[:, 0:1], 0.0), wr=[("ub", part)])
                if (t0 + TB) in ends:
                    tr.op("dve", lambda e: e.memset(ub[:, TB + 1:TB + 2], 0.0), wr=[("ub", part)])
                c0 = 1 - (t0 - lo)
                tr.dma(ub[:, c0:c0 + hi - lo], UT[:, lo:hi], wr=[("ub", part)])
                cw = b.hycw
                col = slice(cc + part * KC, cc + part * KC + 1)
                o = b.cbuf[part]
                tr.op("dve", lambda e: e.tensor_scalar(out=o[:, :], in0=ub[:, 1:TB + 1], scalar1=cw[:, lh, 1, col], scalar2=b.hyb[:, lh, 1, col],
                                                       op0=ALU.mult, op1=ALU.add), rd=[("ub", part)], wr=[("cb", part)])
                tr.op("dve", lambda e: e.scalar_tensor_tensor(out=o[:, :], in0=ub[:, 0:TB], scalar=cw[:, lh, 0, col], in1=o[:, :],
                                                              op0=ALU.mult, op1=ALU.add), rd=[("ub", part), ("cb", part)], wr=[("cb", part)])
                tr.op("dve", lambda e: e.scalar_tensor_tensor(out=o[:, :], in0=ub[:, 2:TB + 2], scalar=cw[:, lh, 2, col], in1=o[:, :],
                                                              op0=ALU.mult, op1=ALU.add), rd=[("ub", part), ("cb", part)], wr=[("cb", part)])
            tr.op("pool", lambda e: e.tensor_tensor(out=b.zbf[:, :], in0=b.cbuf[2][:, :], in1=b.cbuf[1][:, :], op=ALU.mult), rd=[("cb", 1), ("cb", 2)], wr=["zbf"])
            hf, cr = divmod(cc, KC // 2)
            tr.dma(d[f"ZT{hf}"][cr * 128:(cr + 1) * 128, t0:t0 + TB], b.zbf[:, :], rd=["zbf"])
            tr.dma(d["X0T"][cc * 128:(cc + 1) * 128, t0:t0 + TB], b.cbuf[0][:, :], rd=[("cb", 0)])
        tr.end_iter()


def hy_filters(b, tr, lh, Ls):
    nc = b.nc
    d = b.dram
    Q = "sp"
    tr.op("dve", lambda e: e.memset(b.fw4[:, :], 0.0), wr=["fw4"])
    tr.op("dve", lambda e: e.memset(b.zf[:, :], 0.0), wr=["zf"])
    tr.dma(b.fw4[0:64, :], d["hy_filt_w4"][lh], wr=["fw4"])
    tr.end_iter()
    for L in Ls:
        ZF = d[f"ZF{L}"]
        TP = d[f"TP{L}"]
        K2v = [d[f"K2_{L}_{hf}"].rearrange("(c p) m -> p c m", p=128) for hf in range(2)]
        for it in range(L // FT):
            for half in range(2):
                m0 = it * FT if half == 0 else it * FT + L
                tr.dma(b.zf[0:33, :], ZF[:, m0:m0 + FT], wr=["zf"], q=Q)
                tr.dma(b.tp[:, :], TP[:, m0:m0 + FT], wr=["tp"], q=Q)
                src = b.zf[:, :]
                srck = "zf"
                for layer in range(3):
                    wl = b.fw1[:, lh, :] if layer == 0 else b.fw23[:, lh, layer - 1, :]
                    ps = b.pbank[layer % 2]
                    pk = ("fps", layer % 2)
                    tr.mm(ps[:, :], [(wl, src)], rd=[srck], wr=[pk])
                    y = b.fy
                    tr.op("dve", lambda e: e.tensor_scalar(out=y[:, :], in0=ps[:, :], scalar1=b.fb[:, lh, layer:layer + 1], scalar2=b.ffr[:, lh:lh + 1],
                                                           op0=ALU.add, op1=ALU.mult), rd=[pk], wr=["fy"])
                    for rep in range(2):
                        tr.op("dve", lambda e: e.tensor_scalar(out=b.fm[:, :], in0=y[:, :], scalar1=float(np.pi), scalar2=-TWO_PI, op0=ALU.is_gt, op1=ALU.mult), rd=["fy"], wr=["fm"])
                        tr.op("dve", lambda e: e.tensor_tensor(out=y[:, :], in0=y[:, :], in1=b.fm[:, :], op=ALU.add), rd=["fm", "fy"], wr=["fy"])
                        tr.op("dve", lambda e: e.tensor_scalar(out=b.fm[:, :], in0=y[:, :], scalar1=-float(np.pi), scalar2=TWO_PI, op0=ALU.is_lt, op1=ALU.mult), rd=["fy"], wr=["fm"])
                        tr.op("dve", lambda e: e.tensor_tensor(out=y[:, :], in0=y[:, :], in1=b.fm[:, :], op=ALU.add), rd=["fm", "fy"], wr=["fy"])
                    hbuf = b.fh[layer % 2]
                    tr.op("act", lambda e: e.activation(out=hbuf[:, :], in_=y[:, :], func=AF.Sin), rd=["fy"], wr=[("fh", layer % 2)])
                    src = hbuf[:, :]
                    srck = ("fh", layer % 2)
                kst = b.kstage[half]
                for cc in range(KC):
                    ch0 = (0 if half == 1 else D) + cc * 128
                    s = cc % 2
                    ps = b.pbank[2 + s]
                    tr.mm(ps[:, :], [(b.fw4[:, ch0:ch0 + 128], src)], rd=[srck], wr=[("kps", s)])
                    tr.op("act", lambda e: e.activation(out=b.rbuf[s][:, :], in_=b.tp[:, :], func=AF.Exp, scale=b.negdelta[:, cc:cc + 1]), rd=["tp"], wr=[("dec", s)])
                    tr.op("dve", lambda e: e.tensor_tensor(out=kst[:, cc, :], in0=ps[:, :], in1=b.rbuf[s][:, :], op=ALU.mult), rd=[("kps", s), ("dec", s)], wr=[("kst", half)])
                for hf in range(2):
                    tr.dma(K2v[hf][:, :, m0:m0 + FT], kst[:, hf * 8:(hf + 1) * 8, :], rd=[("kst", half)], q=Q)
            tr.end_iter()


NCH = 2


def hy_longconv(b, tr, groups):
    nc = b.nc
    d = b.dram
    Q = "act"
    ntok = b.ntok
    with fori(b, D // NCH) as ci:
        for cl in range(NCH):
            for gi, (L, s0, ns) in enumerate(groups):
                nb = L // 128
                glen = L + 127
                gsrc = bass.AP(d[f"K2_{L}_{cl}"].tensor, ci * (2 * L) + 1, [[128, nb], [1, glen]])
                tr.dma(b.gbuf[cl][gi][0:nb, 0:glen], gsrc, wr=[("g", cl, gi)], q=Q)
                zsrc = bass.AP(d[f"ZT{cl}"].tensor, ci * ntok + s0, [[128, nb], [L, ns], [1, 128]])
                tr.dma(b.zc[cl][gi][0:nb, 0:ns, :], zsrc, wr=[("zc", cl, gi)], q=Q)
        for cl in range(NCH):
            for gi, (L, s0, ns) in enumerate(groups):
                nb = L // 128
                gb = b.gbuf[cl][gi]
                zc = b.zc[cl][gi]
                zr = b.zr[gi]
                J = b.j128 if nb == 128 else b.j16
                pz = b.pbank[1]
                pz3 = pz[0:nb, 0:ns * 128].rearrange("p (a t) -> p a t", a=ns)
                tr.mm(pz3, [(J[0:nb, 0:nb], zc[0:nb, 0:ns, :])], rd=[("zc", cl, gi)], wr=["pz"])
                tr.op("dve", lambda e: e.tensor_copy(out=zr[0:nb, 0:ns * 128], in_=pz[0:nb, 0:ns * 128]), rd=["pz"], wr=[("zr", gi)])
                ps = b.pbank[2 + (gi % 2)]
                pk = ("yps", gi % 2)
                us = [0] + [u for u in range(-127, 128) if u != 0]
                tr._deps("pe", [("g", cl, gi), ("zr", gi)], [pk])
                zr3 = zr[0:nb, 0:ns * 128].rearrange("p (a t) -> p a t", a=ns)
                ps3 = ps[0:nb, 0:ns * 128].rearrange("p (a t) -> p a t", a=ns)
                for idx, u in enumerate(us):
                    N = 128 - abs(u)
                    tlo = max(0, u)
                    slo = max(0, -u)
                    lhsT = gb[0:nb, u + 127:u + 127 + 128 * (nb - 1) + 1:128]
                    ins = nc.tensor.matmul(ps3[:, :, tlo:tlo + N], lhsT=lhsT, rhs=zr3[:, :, slo:slo + N],
                                           start=(idx == 0), stop=(idx == len(us) - 1), skip_group_check=True)
                cnt = tr.esem["pe"].inc(ins)
                tr._reg((tr.esem["pe"], cnt), [("g", cl, gi), ("zr", gi)], [pk])
                yst = b.yst[gi]
                tr.op("dve", lambda e: e.tensor_copy(out=yst[0:nb, 0:ns * 128], in_=ps[0:nb, 0:ns * 128]), rd=[pk], wr=[("yst", gi)])
                ydst = bass.AP(d[f"YC{cl}"].tensor, ci * ntok + s0, [[128, nb], [L, ns], [1, 128]])
                tr.dma(ydst, yst[0:nb, 0:ns * 128].rearrange("p (a t) -> p a t", a=ns), rd=[("yst", gi)], q=Q)
        tr.end_iter()


def hy_gate(b, tr, lh):
    d = b.dram
    for cc in range(KC):
        for t0 in range(0, b.ntok, TB):
            row = slice(cc * 128, (cc + 1) * 128)
            hf, cr = divmod(cc, KC // 2)
            rowh = slice(cr * 128, (cr + 1) * 128)
            tr.dma(b.cbuf[0][:, :], d[f"YC{hf}"][rowh, t0:t0 + TB], wr=[("cb", 0)])
            tr.dma(b.cbuf[1][:, :], d["X0T"][row, t0:t0 + TB], wr=[("cb", 1)])
            tr.dma(b.zbf[:, :], d[f"ZT{hf}"][rowh, t0:t0 + TB], wr=["zbf"])
            tr.op("dve", lambda e: e.scalar_tensor_tensor(out=b.cbuf[0][:, :], in0=b.zbf[:, :], scalar=b.hysk[:, lh, cc:cc + 1], in1=b.cbuf[0][:, :],
                                                          op0=ALU.mult, op1=ALU.add), rd=["zbf", ("cb", 0)], wr=[("cb", 0)])
            tr.op("dve", lambda e: e.tensor_tensor(out=b.ybf[:, :], in0=b.cbuf[0][:, :], in1=b.cbuf[1][:, :], op=ALU.mult), rd=[("cb", 0), ("cb", 1)], wr=["ybf"])
            tr.dma(d[f"OT{cc}"][:, t0:t0 + TB], b.ybf[:, :], rd=["ybf"])
        tr.end_iter()


def bufspec(name):
    f = lambda: ([128, T], F32)
    S = {
        "x_sb": ([128, KC, T], F32), "xn": ([128, KC, T], BF16), "h_sb": ([128, DFF // 128, T], BF16), "stage": 2 * [([128, T], BF16)],
        "wbuf": 3 * [([128, WBLK], BF16)], "sq": 2 * [f()], "rbuf": 2 * [f()], "rt": f(), "rstd": f(),
        "ropec": f(), "ropes": f(), "qg": 2 * [f()], "t1": f(), "t2": f(),
        "anum": 2 * [([128, SEG], F32)], "aden": 2 * [([128, SEG], F32)], "obf": 2 * [([128, SEG], BF16)], "smask": ([128, 256], BF16),
        "pexp": 2 * [([128, 256], F32)], "pm": 2 * [([128, 256], BF16)], "vblk": 2 * [([128, 128], BF16)],
        "ubuf": 3 * [([128, TB + 2], F32)], "cbuf": 3 * [([128, TB], F32)], "zbf": ([128, TB], BF16), "ybf": ([128, TB], BF16),
        "zf": ([128, FT], F32), "tp": ([128, FT], F32), "fy": ([128, FT], F32), "fm": ([128, FT], F32),
        "fh": 2 * [([128, FT], F32)], "fw4": ([128, 2 * D], F32), "kstage": 2 * [([128, KC, FT], BF16)],
        "zr": 2 * [([128, 256], BF16)], "yst": 2 * [([128, 256], F32)],
        "zpad": ([128, PAD], BF16), "cf": 2 * [([128, WBLK], F32)], "cbf": 2 * [([128, WBLK], BF16)],
    }
    return S[name]


class Phase:
    def __init__(self, b, names, extra=None):
        self.b = b
        self.names = names
        self.extra = extra

    def __enter__(self):
        self.st = ExitStack()
        b = self.b
        mk = lambda nm, sh, dt: self.st.enter_context(b.nc.sbuf_tensor(f"ph_{nm}_{b.nc.next_id()}", sh, dt))
        for n in self.names:
            spec = bufspec(n)
            if isinstance(spec, list):
                setattr(b, n, [mk(f"{n}{i}", sh, dt) for i, (sh, dt) in enumerate(spec)])
            else:
                setattr(b, n, mk(n, spec[0], spec[1]))
        if self.extra:
            self.extra(mk)
        return self

    def __exit__(self, *a):
        self.st.close()
        return False


TOK = ["x_sb", "xn", "h_sb", "wbuf", "sq", "rbuf", "rt", "rstd", "ropec", "ropes", "qg", "t1", "t2", "stage"]


def build_program(ntok, seqs, depth=4, stop=None):
    b = Builder(ntok)
    nc = b.nc
    NT2 = ntok + 2 * PAD
    nseg = ntok // SEG
    Ls = sorted({L for (_, L) in seqs}, reverse=True)
    groups = []
    for L in Ls:
        ss = sorted(s0 for (s0, l_) in seqs if l_ == L)
        assert all(ss[i] == ss[0] + i * L for i in range(len(ss))) and len(ss) <= 2
        groups.append((L, ss[0], len(ss)))
    xT = b.din("xT", [D, ntok])
    b.din("mix_norm", [4, D]); b.din("mlp_norm", [4, D])
    wqkv = b.din("attn_w_qkv", [2, D, QKVW]); b.din("attn_q_gain", [2, 3, HD]); b.din("attn_k_gain", [2, 3, HD])
    wao = b.din("attn_w_out", [2, D, D])
    hwin = b.din("hy_w_in", [2, D, 3 * D]); b.din("hy_b_in", [2, 3 * D]); b.din("hy_conv_w", [2, 3, 3 * D]); b.din("hy_conv_b", [2, 3 * D])
    b.din("hy_filt_w1", [2, 33, 64]); b.din("hy_filt_b1", [2, 64]); b.din("hy_filt_w2", [2, 64, 64]); b.din("hy_filt_b2", [2, 64])
    b.din("hy_filt_w3", [2, 64, 64]); b.din("hy_filt_b3", [2, 64]); b.din("hy_filt_freq", [2, 64]); b.din("hy_filt_w4", [2, 64, 2 * D])
    b.din("hy_skip", [2, D]); hwo = b.din("hy_w_out", [2, D, D]); b.din("hy_b_out", [2, D])
    w1 = b.din("mlp_w1", [4, D, DFF]); w2 = b.din("mlp_w2", [4, DFF, D])
    b.din("ropeC", [128, ntok]); b.din("ropeS", [128, ntok]); b.din("masks", [128, 4, 128]); b.din("perm", [128, 128])
    b.din("negdelta", [128, KC]); b.din("MASKT", [nseg, 128, 2, 128], BF16 if False else F32); b.din("J128", [128, 128]); b.din("J16", [16, 16])
    for L in Ls:
        b.din(f"ZF{L}", [33, 2 * L]); b.din(f"TP{L}", [128, 2 * L])
        for hf in range(2):
            b.dint(f"K2_{L}_{hf}", [D // 2, 2 * L], BF16)
    b.XT = b.dout("yT", [D, ntok])
    d = b.dram

    def blocked(name, K, N, nl):
        nk, cb, nblk = wblocks(K, N)
        return [b.dint(f"{name}{l}", [nblk, 128, WBLK], BF16) for l in range(nl)]
    wqkvb = blocked("wqkvb", D, QKVW, 2); waob = blocked("waob", D, D, 2)
    hwinb = blocked("hwinb", D, 3 * D, 2); hwob = blocked("hwob", D, D, 2)
    w1b = blocked("w1b", D, DFF, 4); w2b = blocked("w2b", DFF, D, 4)
    for i in range(QKVW // 128):
        b.dint(f"QKV{i}", [128, NT2], BF16)
    for i in range(3 * KC):
        b.dint(f"UT{i}", [128, ntok], F32)
    for i in range(KC):
        b.dint(f"OT{i}", [128, ntok], BF16)
    for hf in range(2):
        b.dint(f"ZT{hf}", [D // 2, ntok], BF16); b.dint(f"YC{hf}", [D // 2, ntok], F32)
    b.dint("X0T", [D, ntok], F32)
    b.dint("MASKTB", [nseg, 128, 256], BF16)
    with b.stack:
        sb = b.sb
        b.ones_f = sb("ones_f", [128, 128], F32); b.ones_b = sb("ones_b", [128, 128], BF16)
        b.perm_f = sb("perm_f", [128, 128], F32); b.ident_b = sb("ident_b", [128, 128], BF16)
        b.j128 = sb("j128", [128, 128], BF16); b.j16 = sb("j16", [16, 16], BF16)
        b.eps_c = sb("eps_c", [128, 1], F32)
        b.gmix = sb("gmix", [128, 4, KC], F32); b.gmlp = sb("gmlp", [128, 4, KC], F32)
        b.qkgain = sb("qkgain", [128, 2, 6], F32); b.mask = sb("mask", [128, 4, 128], BF16)
        b.hyb = sb("hyb", [128, 2, 2, 48], F32); b.hycw = sb("hycw", [128, 2, 3, 48], F32)
        b.hysk = sb("hysk", [128, 2, KC], F32); b.hybo = sb("hybo", [128, 2, KC], F32); b.negdelta = sb("negdelta", [128, KC], F32)
        b.fw1 = sb("fw1", [128, 2, 128], F32); b.fw23 = sb("fw23", [128, 2, 2, 128], F32); b.fb = sb("fb", [128, 2, 3], F32); b.ffr = sb("ffr", [128, 2], F32)
        b.pbank = [b.ps(f"pb{i}", [128, 512]) for i in range(4)]
        b.ps_ss = b.ps("ps_ss", [128, 512]); b.ps_rq = b.ps("ps_rq", [128, 512])
        b.ps_vt = [b.ps(f"ps_vt{i}", [128, 1024], BF16) for i in range(2)]
        b.ps_s = b.pbank[0:2]; b.ps_num = b.pbank[2:4]; b.ps_den = [b.ps_ss, b.ps_rq]
        b.s_cast = b.sem("s_cast")
        tr = Tracker(b)
        with Phase(b, ["zpad"], None) as ph:
            maskf = ph.st.enter_context(nc.sbuf_tensor("maskf", [128, 4, 128], F32))
            jf = ph.st.enter_context(nc.sbuf_tensor("jf", [128, 128], F32))
            jf16 = ph.st.enter_context(nc.sbuf_tensor("jf16", [16, 16], F32))
            mt = ph.st.enter_context(nc.sbuf_tensor("mt", [128, 256], F32))
            mtb = ph.st.enter_context(nc.sbuf_tensor("mtb", [128, 256], BF16))
            nc.vector.memset(b.ones_f[:, :], 1.0); nc.vector.memset(b.ones_b[:, :], 1.0); nc.vector.memset(b.eps_c[:, :], EPS)
            nc.vector.memset(b.zpad[:, :], 0.0)
            nc.vector.memset(b.fw1[:, :, :], 0.0); nc.vector.memset(b.fw23[:, :, :, :], 0.0)
            nc.vector.memset(b.fb[:, :, :], 0.0); nc.vector.memset(b.ffr[:, :], 0.0)
            nc.all_engine_barrier()
            ns = dict(allow_slow_non_contiguous=True)
            tr.dma(b.gmix[:, :, :], d["mix_norm"].rearrange("l (c p) -> p l c", p=128), wr=["c0"], **ns)
            tr.dma(b.gmlp[:, :, :], d["mlp_norm"].rearrange("l (c p) -> p l c", p=128), wr=["c1"], **ns)
            for la in range(2):
                tr.dma(b.qkgain[:, la, 0:6:2], d["attn_q_gain"][la].rearrange("g p -> p g"), wr=["c2"], **ns)
                tr.dma(b.qkgain[:, la, 1:6:2], d["attn_k_gain"][la].rearrange("g p -> p g"), wr=["c2"], **ns)
            tr.dma(maskf[:, :, :], d["masks"][:, :, :], wr=["maskf"])
            tr.dma(jf[:, :], d["J128"][:, :], wr=["jf"])
            tr.dma(jf16[:, :], d["J16"][:, :], wr=["jf16"])
            tr.dma(b.perm_f[:, :], d["perm"][:, :], wr=["c3"])
            tr.dma(b.negdelta[:, :], d["negdelta"][:, :], wr=["c4"])
            for lh in range(2):
                tr.dma(b.hyb[:, lh, 0, :], d["hy_b_in"][lh:lh + 1, :].rearrange("o (c p) -> p (o c)", p=128), wr=["c5"], **ns)
                tr.dma(b.hyb[:, lh, 1, :], d["hy_conv_b"][lh:lh + 1, :].rearrange("o (c p) -> p (o c)", p=128), wr=["c5"], **ns)
                for k_ in range(3):
                    tr.dma(b.hycw[:, lh, k_, :], d["hy_conv_w"][lh, k_:k_ + 1, :].rearrange("o (c p) -> p (o c)", p=128), wr=["c6"], **ns)
                tr.dma(b.fw1[0:33, lh, 0:64], d["hy_filt_w1"][lh], wr=["c7"])
                tr.dma(b.fw23[0:64, lh, 0, 0:64], d["hy_filt_w2"][lh], wr=["c7"])
                tr.dma(b.fw23[0:64, lh, 1, 0:64], d["hy_filt_w3"][lh], wr=["c7"])
                for k_, nm in enumerate(["hy_filt_b1", "hy_filt_b2", "hy_filt_b3"]):
                    tr.dma(b.fb[0:64, lh, k_:k_ + 1], d[nm][lh:lh + 1, :].rearrange("o p -> p o"), wr=["c8"], **ns)
                tr.dma(b.ffr[0:64, lh:lh + 1], d["hy_filt_freq"][lh:lh + 1, :].rearrange("o p -> p o"), wr=["c8"], **ns)
            tr.dma(b.hysk[:, :, :], d["hy_skip"].rearrange("l (c p) -> p l c", p=128), wr=["c9"], **ns)
            tr.dma(b.hybo[:, :, :], d["hy_b_out"].rearrange("l (c p) -> p l c", p=128), wr=["c9"], **ns)
            tr.op("dve", lambda e: e.tensor_copy(out=b.mask[:, :, :], in_=maskf[:, :, :]), rd=["maskf"], wr=["mask"])
            tr.op("dve", lambda e: e.tensor_tensor(out=b.ident_b[:, :], in0=maskf[:, 0, :], in1=maskf[:, 1, :], op=ALU.mult), rd=["maskf"], wr=["ident"])
            tr.op("dve", lambda e: e.tensor_copy(out=b.j128[:, :], in_=jf[:, :]), rd=["jf"], wr=["j128"])
            tr.op("dve", lambda e: e.tensor_copy(out=b.j16[:, :], in_=jf16[:, :]), rd=["jf16"], wr=["j16"])
            tr.op("dve", lambda e: e.tensor_scalar(out=b.qkgain[:, :, 0:6:2], in0=b.qkgain[:, :, 0:6:2], scalar1=float(HD ** -0.5), scalar2=None, op0=ALU.mult), rd=["c2"], wr=["c2"])
            for sgi in range(nseg):
                tr.dma(mt[:, :], d["MASKT"][sgi].rearrange("p a b -> p (a b)"), wr=["mt"])
                tr.op("dve", lambda e: e.tensor_copy(out=mtb[:, :], in_=mt[:, :]), rd=["mt"], wr=["mtb"])
                tr.dma(d["MASKTB"][sgi], mtb[:, :], rd=["mtb"])
            for g3 in range(NG):
                for a in (1, 2):
                    for h in range(NH):
                        tv = d[f"QKV{(g3 * 3 + a) * NH + h}"]
                        tr.dma(tv[:, 0:PAD], b.zpad[:, :], rd=["zp"])
                        tr.dma(tv[:, PAD + ntok:NT2], b.zpad[:, :], rd=["zp"])
            tr.end_iter()
        b.cast_ctr = 0
        with Phase(b, ["cf", "cbf"]):
            for l in range(2):
                cast_blocked(b, tr, wqkvb[l], wqkv[l], D, QKVW); cast_blocked(b, tr, waob[l], wao[l], D, D)
                cast_blocked(b, tr, hwinb[l], hwin[l], D, 3 * D); cast_blocked(b, tr, hwob[l], hwo[l], D, D)
            for l in range(4):
                cast_blocked(b, tr, w1b[l], w1[l], D, DFF); cast_blocked(b, tr, w2b[l], w2[l], DFF, D)
            tr.end_iter()
        xsrc = xT
        pre = None
        mlp = None
        for layer in range(depth + 1):
            j = layer // 2
            if layer == depth:
                post = None
            elif layer % 2 == 0:
                post = ("qkv", wqkvb[j], b.gmix[:, layer, :], j)
            else:
                post = ("hyin", hwinb[j], b.gmix[:, layer, :], j)
            with Phase(b, TOK):
                token_loop(b, tr, xsrc, pre, mlp, post)
            xsrc = b.XT
            if layer == depth or stop == f"tok{layer}":
                break
            if layer % 2 == 0:
                def extra_a(mk):
                    b.qT = [[mk(f"qT{p}{g}", [128, SEG], BF16) for g in range(NG)] for p in range(2)]
                    b.kT = [[mk(f"kT{p}{g}", [128, SEG + 2 * PAD], BF16) for g in range(NG)] for p in range(2)]
                    b.vT = [[mk(f"vT{p}{g}", [128, SEG + 2 * PAD], BF16) for g in range(NG)] for p in range(2)]
                with Phase(b, ["anum", "aden", "obf", "smask", "pexp", "pm", "vblk"], extra_a):
                    attn_core(b, tr, nseg)
                if stop == "attn0":
                    break
                pre = (waob[j], None)
            else:
                with Phase(b, ["ubuf", "cbuf", "zbf"]):
                    hy_conv_gate(b, tr, seqs, j)
                if stop == "hcg":
                    break
                with Phase(b, ["zf", "tp", "fy", "fm", "fh", "rbuf", "kstage", "fw4"]):
                    hy_filters(b, tr, j, Ls)
                if stop == "hfil":
                    break

                def extra(mk):
                    b.gbuf = [[mk(f"g{cl}_{gi}", [L // 128, L + 128], BF16) for gi, (L, _, _) in enumerate(groups)] for cl in range(NCH)]
                    b.zc = [[mk(f"zc{cl}_{gi}", [L // 128, 2, 128], BF16) for gi, (L, _, _) in enumerate(groups)] for cl in range(NCH)]
                with Phase(b, ["zr", "yst"], extra):
                    hy_longconv(b, tr, groups)
                if stop == "hlc":
                    break
                with Phase(b, ["cbuf", "zbf", "ybf"]):
                    hy_gate(b, tr, j)
                pre = (hwob[j], b.hybo[:, j, :])
            mlp = (w1b[layer], w2b[layer], b.gmlp[:, layer, :])
    return nc


def const_tables(seqs, ntok):
    half = HD // 2
    inv = (10000.0 ** (-np.arange(0, HD, 2, dtype=np.float32) / HD)).astype(np.float32)
    ropeC = np.zeros((128, ntok), np.float32); ropeS = np.zeros((128, ntok), np.float32)
    nseg = ntok // SEG
    k = np.arange(128)[:, None]; j = np.arange(128)[None, :]
    mB, mA = (k <= j), (k >= j)
    mA0, mBl = mA & (k >= 64), mB & (k < 64)
    masks = np.stack([mB, mA, mA0, mBl], axis=1).astype(np.float32)
    maskt = np.zeros((nseg, 128, 2, 128), np.float32)
    for sg in range(nseg):
        maskt[sg, :, 0, :] = mA
        maskt[sg, :, 1, :] = mB
    for (s0, L) in seqs:
        ang = np.arange(L, dtype=np.float32)[:, None] * inv[None, :]
        c = np.cos(ang).T.astype(np.float32); s = np.sin(ang).T.astype(np.float32)
        ropeC[:half, s0:s0 + L] = c; ropeC[half:, s0:s0 + L] = c
        ropeS[:half, s0:s0 + L] = -s; ropeS[half:, s0:s0 + L] = s
        maskt[s0 // SEG, :, 0, :] = mA0
        maskt[(s0 + L) // SEG - 1, :, 1, :] = mBl
    perm = np.zeros((128, 128), np.float32)
    perm[np.arange(128), (np.arange(128) + 64) % 128] = 1.0
    deltas = np.abs(np.linspace(np.log(1e-2) / 1.5, np.log(1e-2) / 0.3, D, dtype=np.float32))
    negdelta = np.ascontiguousarray((-deltas).reshape(KC, 128).T).astype(np.float32)
    out = {"ropeC": ropeC, "ropeS": ropeS, "masks": np.ascontiguousarray(masks), "perm": perm, "negdelta": negdelta,
           "MASKT": maskt, "J128": np.ascontiguousarray(np.eye(128, dtype=np.float32)[::-1]),
           "J16": np.ascontiguousarray(np.eye(16, dtype=np.float32)[::-1])}
    for L in sorted({L for (_, L) in seqs}):
        m = np.arange(2 * L)
        pos = np.where(m >= L, m - L, L - m).astype(np.float32)
        t = pos / np.float32(L - 1)
        bands = np.linspace(1e-4, 15, 16, dtype=np.float32)
        ang = (np.float32(2.0 * np.pi / L) * pos[:, None] * bands[None, :]).astype(np.float32)
        z = np.concatenate([t[:, None], np.cos(ang), -np.sin(ang)], axis=-1).astype(np.float32)
        out[f"ZF{L}"] = np.ascontiguousarray(z.T)
        out[f"TP{L}"] = np.ascontiguousarray(np.broadcast_to(t[None, :], (128, 2 * L))).astype(np.float32)
    return out


WNAMES = ["mix_norm", "mlp_norm", "attn_w_qkv", "attn_q_gain", "attn_k_gain", "attn_w_out", "hy_w_in", "hy_b_in", "hy_conv_w",
          "hy_conv_b", "hy_filt_w1", "hy_filt_b1", "hy_filt_w2", "hy_filt_b2", "hy_filt_w3", "hy_filt_b3", "hy_filt_freq",
          "hy_filt_w4", "hy_skip", "hy_w_out", "hy_b_out", "mlp_w1", "mlp_w2"]


def kernel(x_prompt, x_sample, **w):
    seqs = [(0, 16384), (16384, 2048), (18432, 2048)]
    ntok = 20480
    nc = build_program(ntok, seqs)
    tabs = const_tables(seqs, ntok)
    weights = {k: np.ascontiguousarray(np.asarray(w[k], dtype=np.float32)) for k in WNAMES}
    xp = np.asarray(x_prompt); xs = np.asarray(x_sample)
    in_maps = []
    for c in range(8):
        if c < 2:
            xc = np.concatenate([xs[c], xp[2 * c], xp[2 * c + 1]], axis=0)
            xT = np.ascontiguousarray(xc.T)
        else:
            xT = np.zeros((D, ntok), np.float32)
        m = {"xT": xT}
        m.update(weights); m.update(tabs)
        in_maps.append(m)
    res = run_bass_kernel_spmd(nc, in_maps, core_ids=list(range(8)))
    yp = np.zeros(xp.shape, np.float32); ys = np.zeros(xs.shape, np.float32)
    for c in range(2):
        y = res.results[c]["yT"].T
        ys[c] = y[0:16384]; yp[2 * c] = y[16384:18432]; yp[2 * c + 1] = y[18432:20480]
    return (yp, ys)
```

```python
import numpy as np
from contextlib import ExitStack
import concourse.bass as bass
import concourse.mybir as mybir
from concourse.bass_utils import run_bass_kernel_spmd

F32 = mybir.dt.float32
BF16 = mybir.dt.bfloat16
AF = mybir.ActivationFunctionType
ALU = mybir.AluOpType

D = 2048
KC = D // 128
DFF = 4 * D
T = 512
EPS = 1e-6
HD = 128
NH = 16
NG = 3
DILS = (1, 4, 16)
QKVW = NG * 3 * NH * HD
PAD = 1024
SEG = 2048
TB = 2048
WBLK = 8192
FT = 512
TWO_PI = 2.0 * np.pi


class Sem:
    def __init__(self, b, name):
        self.h = b.stack.enter_context(b.nc.semaphore(name))
        self.n = 0
        b.sems.append(self)

    def inc(self, ins, k=1):
        ins.then_inc(self.h, k)
        self.n += k
        return self.n

    def dma(self, ins):
        return self.inc(ins, 16)


class Builder:
    def __init__(self, ntok):
        self.ntok = ntok
        self.nc = bass.Bass("TRN2", target_bir_lowering=False)
        self.stack = ExitStack()
        self.sems = []
        self.dram = {}

    def din(self, name, shape, dt=F32):
        t = self.nc.dram_tensor(name, list(shape), dt, kind="ExternalInput").ap()
        self.dram[name] = t
        return t

    def dout(self, name, shape, dt=F32):
        t = self.nc.dram_tensor(name, list(shape), dt, kind="ExternalOutput").ap()
        self.dram[name] = t
        return t

    def dint(self, name, shape, dt):
        t = self.nc.dram_tensor(name, list(shape), dt).ap()
        self.dram[name] = t
        return t

    def sb(self, name, shape, dt):
        return self.stack.enter_context(self.nc.sbuf_tensor("sb_" + name, list(shape), dt))

    def ps(self, name, shape, dt=F32):
        return self.stack.enter_context(self.nc.psum_tensor(name, list(shape), dt))

    def sem(self, name):
        return Sem(self, name)

    def end_iter(self):
        nc = self.nc
        nc.all_engine_barrier()
        for s in self.sems:
            if s.n:
                nc.gpsimd.sem_clear(s.h)
                s.n = 0
        nc.all_engine_barrier()


from contextlib import contextmanager


@contextmanager
def fori(b, end):
    nc = b.nc
    if getattr(b, "loopregs", None) is None:
        b.loopregs = nc.alloc_registers("lp_shared", engines=mybir.ALL_ENGINES)
    regs = b.loopregs
    lid = nc.next_id()
    ls = f"myfori_{lid}_loop"
    le = f"myfori_{lid}_end"
    nc.regs_mov(regs, 0)
    nc.br(ls, engines=mybir.ALL_ENGINES)
    with nc.body(ls, valid_engines=mybir.ALL_ENGINES):
        yield nc.snap(regs, min_val=0, max_val=end - 1)
        nc.regs_alu(regs, regs, 1, op=mybir.AluOpType.add)
        nc.br_lt(regs, end, on_true=ls, on_false=le, engines=mybir.ALL_ENGINES)
    nc.switch_bb(le)


class Tracker:
    def __init__(self, b, ndma=8):
        nc = b.nc
        self.b = b
        self.eng = {"pe": nc.tensor, "act": nc.scalar, "dve": nc.vector, "pool": nc.gpsimd, "sp": nc.sync}
        self.esem = {e: b.sem("t_" + e) for e in ("pe", "act", "dve", "pool")}
        self.slots = [b.sem(f"t_dma{i}") for i in range(ndma)]
        self.reset()

    def reset(self):
        self.waited = {e: {} for e in self.eng}
        self.lastw = {}
        self.readers = {}
        self.nslot = 0

    def _wait(self, e, dep):
        sem, cnt = dep
        if self.waited[e].get(sem, 0) >= cnt:
            return
        self.eng[e].wait_ge(sem.h, cnt)
        self.waited[e][sem] = cnt

    def _deps(self, e, rd, wr):
        for k in rd:
            if k in self.lastw:
                self._wait(e, self.lastw[k])
        for k in wr:
            if k in self.lastw:
                self._wait(e, self.lastw[k])
            for r in self.readers.get(k, ()):
                self._wait(e, r)

    def _reg(self, dep, rd, wr):
        for k in rd:
            self.readers.setdefault(k, []).append(dep)
        for k in wr:
            self.lastw[k] = dep
            self.readers[k] = []

    def op(self, e, fn, rd=(), wr=()):
        self._deps(e, rd, wr)
        ins = fn(self.eng[e])
        cnt = self.esem[e].inc(ins)
        self._reg((self.esem[e], cnt), rd, wr)
        return ins

    def mm(self, out, ops, rd=(), wr=(), **kw):
        self._deps("pe", rd, wr)
        n = len(ops)
        for i, (l, r) in enumerate(ops):
            ins = self.b.nc.tensor.matmul(out, lhsT=l, rhs=r, start=(i == 0), stop=(i == n - 1), **kw)
        cnt = self.esem["pe"].inc(ins)
        self._reg((self.esem["pe"], cnt), rd, wr)

    def dma(self, out, in_, rd=(), wr=(), q="sp", **kw):
        slot = self.slots[self.nslot % len(self.slots)]
        self.nslot += 1
        if slot.n:
            self._wait(q, (slot, slot.n))
        self._deps(q, rd, wr)
        ins = self.eng[q].dma_start(out=out, in_=in_, **kw)
        cnt = slot.dma(ins)
        self._reg((slot, cnt), rd, wr)

    def maybe_reset(self, lim=600):
        if max(s.n for s in list(self.esem.values()) + self.slots) > lim:
            self.end_iter()
            return True
        return False

    def end_iter(self):
        for k, dep in list(self.lastw.items()):
            self._wait("sp", dep)
        for k, rs in list(self.readers.items()):
            for r in rs:
                self._wait("sp", r)
        self.b.end_iter()
        self.reset()


def wblocks(K, N):
    nk = K // 128
    cb = WBLK // nk
    return nk, cb, N // cb


def cast_blocked(b, tr, dst, src, K, N):
    nk, cb, nblk = wblocks(K, N)
    for j in range(nblk):
        tr.maybe_reset(300)
        i = b.cast_ctr % 2
        b.cast_ctr += 1
        sv = src[:, j * cb:(j + 1) * cb].rearrange("(kc p) n -> p kc n", p=128)
        fv = b.cf[i][:, :].rearrange("p (kc n) -> p kc n", kc=nk)
        keys = []
        for k0 in range(0, nk, 16):
            tr.dma(fv[:, k0:k0 + 16, :], sv[:, k0:k0 + 16, :], wr=[("cf", i, k0)])
            keys.append(("cf", i, k0))
        e = ("dve", "act", "pool")[b.cast_ctr % 3]
        if e == "act":
            tr.op(e, lambda en: en.activation(out=b.cbf[i][:, :], in_=b.cf[i][:, :], func=AF.Copy), rd=keys, wr=[("cbf", i)])
        else:
            tr.op(e, lambda en: en.tensor_copy(out=b.cbf[i][:, :], in_=b.cf[i][:, :]), rd=keys, wr=[("cbf", i)])
        tr.dma(dst[j], b.cbf[i][:, :], rd=[("cbf", i)])


def linear_tile(b, tr, rhs_of, rd_rhs, Wb, K, N, epi):
    nk, cb, nblk = wblocks(K, N)
    g = 0
    issued = 0
    for j in range(nblk):
        if tr.maybe_reset():
            issued = j
        while issued < min(nblk, j + 2):
            tr.dma(b.wbuf[issued % 3][:, :], Wb[issued], wr=[("w", issued % 3)])
            issued += 1
        wb = j % 3
        wv = b.wbuf[wb][:, :].rearrange("p (c n) -> p c n", c=nk)
        for nl in range(cb // 128):
            n = j * (cb // 128) + nl
            bk = ("bank", g % 4)
            bank = b.pbank[g % 4]
            tr.mm(bank[:, :], [(wv[:, kc, nl * 128:(nl + 1) * 128], rhs_of(kc)) for kc in range(nk)],
                  rd=[("w", wb)] + list(rd_rhs), wr=[bk])
            epi(n, bank, bk)
            g += 1


def norm_tile(b, tr, gvec):
    for c in range(KC):
        q = b.sq[c % 2]
        tr.op("act", lambda e: e.activation(out=q[:, :], in_=b.x_sb[:, c, :], func=AF.Square), rd=["x"], wr=[("sq", c % 2)])
        tr._deps("pe", [("sq", c % 2)], ["ps_ss"] if c == 0 else [])
        ins = b.nc.tensor.matmul(b.ps_ss[:, :], lhsT=b.ones_f[:, :], rhs=q[:, :], start=(c == 0), stop=(c == KC - 1))
        cnt = tr.esem["pe"].inc(ins)
        tr._reg((tr.esem["pe"], cnt), [("sq", c % 2)], ["ps_ss"])
    tr.op("act", lambda e: e.activation(out=b.rt[:, :], in_=b.ps_ss[:, :], func=AF.Sqrt, bias=b.eps_c[:, 0:1], scale=1.0 / D), rd=["ps_ss"], wr=["rt"])
    tr.op("dve", lambda e: e.reciprocal(out=b.rstd[:, :], in_=b.rt[:, :]), rd=["rt"], wr=["rstd"])
    for c in range(KC):
        tr.op("dve", lambda e: e.scalar_tensor_tensor(out=b.xn[:, c, :], in0=b.x_sb[:, c, :], scalar=gvec[:, c:c + 1],
                                                      in1=b.rstd[:, :], op0=ALU.mult, op1=ALU.mult), rd=["x", "rstd"], wr=["xn"])


def token_loop(b, tr, xsrc, pre, mlp, post):
    nc = b.nc
    XTv = b.XT.rearrange("(c p) t -> p c t", p=128)
    xsv = xsrc.rearrange("(c p) t -> p c t", p=128)
    d = b.dram
    h_sb = b.h_sb
    with fori(b, b.ntok // T) as it:
        tok = it * T
        tokp = it * T + PAD
        tr.dma(b.x_sb[:, :, :], xsv[:, :, bass.ds(tok, T)], wr=["x"])
        if pre is not None:
            Wb, bias = pre
            for c in range(KC):
                tr.dma(b.xn[:, c, :], d[f"OT{c}"][:, bass.ds(tok, T)], wr=["xn"])

            def epi0(n, bank, bk):
                if bias is None:
                    tr.op("dve", lambda e: e.tensor_tensor(out=b.x_sb[:, n, :], in0=b.x_sb[:, n, :], in1=bank[:, :], op=ALU.add), rd=[bk, "x"], wr=["x"])
                else:
                    tr.op("dve", lambda e: e.scalar_tensor_tensor(out=b.x_sb[:, n, :], in0=bank[:, :], scalar=bias[:, n:n + 1], in1=b.x_sb[:, n, :],
                                                                  op0=ALU.add, op1=ALU.add), rd=[bk, "x"], wr=["x"])
            linear_tile(b, tr, lambda kc: b.xn[:, kc, :], ["xn"], Wb, D, D, epi0)
        if mlp is not None:
            w1b, w2b, gvec = mlp
            norm_tile(b, tr, gvec)

            def epi1(n, bank, bk):
                s = n % 2
                rb = b.rbuf[s]
                tr.op("act", lambda e: e.activation(out=rb[:, :], in_=bank[:, :], func=AF.Relu), rd=[bk], wr=[("rb", s)])
                tr.op("dve", lambda e: e.tensor_tensor(out=h_sb[:, n, :], in0=rb[:, :], in1=rb[:, :], op=ALU.mult), rd=[("rb", s)], wr=["h"])
            linear_tile(b, tr, lambda kc: b.xn[:, kc, :], ["xn"], w1b, D, DFF, epi1)

            def epi2(n, bank, bk):
                tr.op("dve", lambda e: e.tensor_tensor(out=b.x_sb[:, n, :], in0=b.x_sb[:, n, :], in1=bank[:, :], op=ALU.add), rd=[bk, "x"], wr=["x"])
            linear_tile(b, tr, lambda kc: h_sb[:, kc, :], ["h"], w2b, DFF, D, epi2)
        if pre is not None or mlp is not None or xsrc is not b.XT:
            tr.dma(XTv[:, :, bass.ds(tok, T)], b.x_sb[:, :, :], rd=["x"])
        if post is not None and post[0] == "qkv":
            _, Wb, gvec, la = post
            norm_tile(b, tr, gvec)
            tr.dma(b.ropec[:, :], d["ropeC"][:, bass.ds(tok, T)], wr=["ropec"])
            tr.dma(b.ropes[:, :], d["ropeS"][:, bass.ds(tok, T)], wr=["ropes"])

            def epiq(n, bank, bk):
                g3, rem = divmod(n, 3 * NH)
                a, h = divmod(rem, NH)
                s = n % 2
                st = b.stage[s]
                skey = ("stg", s)
                import os as _os
                if a == 2 or _os.environ.get("QKVSIMPLE"):
                    tr.op("act", lambda e: e.activation(out=st[:, :], in_=bank[:, :], func=AF.Copy), rd=[bk], wr=[skey])
                else:
                    gcol = b.qkgain[:, la, g3 * 2 + a:g3 * 2 + a + 1]
                    sq = b.sq[s]
                    qg = b.qg[s]
                    tr.op("act", lambda e: e.activation(out=sq[:, :], in_=bank[:, :], func=AF.Square), rd=[bk], wr=[("sq", s)])
                    tr.op("act", lambda e: e.activation(out=qg[:, :], in_=bank[:, :], func=AF.Copy, scale=gcol), rd=[bk], wr=[("qg", s)])
                    tr.mm(b.ps_ss[:, :], [(b.ones_f[:, :], sq[:, :])], rd=[("sq", s)], wr=["ps_ss"])
                    tr.mm(b.ps_rq[:, :], [(b.perm_f[:, :], qg[:, :])], rd=[("qg", s)], wr=["ps_rq"])
                    tr.op("act", lambda e: e.activation(out=b.rt[:, :], in_=b.ps_ss[:, :], func=AF.Ln, bias=b.eps_c[:, 0:1], scale=1.0 / HD), rd=["ps_ss"], wr=["rt"])
                    tr.op("act", lambda e: e.activation(out=b.rstd[:, :], in_=b.rt[:, :], func=AF.Exp, scale=-0.5), rd=["rt"], wr=["rstd"])
                    tr.op("dve", lambda e: e.tensor_tensor(out=b.t1[:, :], in0=qg[:, :], in1=b.ropec[:, :], op=ALU.mult), rd=[("qg", s), "ropec"], wr=["t1"])
                    tr.op("dve", lambda e: e.tensor_tensor(out=b.t2[:, :], in0=b.ps_rq[:, :], in1=b.ropes[:, :], op=ALU.mult), rd=["ps_rq", "ropes"], wr=["t2"])
                    tr.op(_os.environ.get("ADDENG", "pool"), lambda e: e.tensor_tensor(out=b.t1[:, :], in0=b.t1[:, :], in1=b.t2[:, :], op=ALU.add), rd=["t1", "t2"], wr=["t1"])
                    tr.op("dve", lambda e: e.tensor_tensor(out=st[:, :], in0=b.t1[:, :], in1=b.rstd[:, :], op=ALU.mult), rd=["t1", "rstd"], wr=[skey])
                tr.dma(d[f"QKV{n}"][:, bass.ds(tokp, T)], st[:, :], rd=[skey])
            linear_tile(b, tr, lambda kc: b.xn[:, kc, :], ["xn"], Wb, D, QKVW, epiq)
        if post is not None and post[0] == "hyin":
            _, Wb, gvec, lh = post
            norm_tile(b, tr, gvec)

            def epih(n, bank, bk):
                s = n % 2
                tr.op("act", lambda e: e.activation(out=b.rbuf[s][:, :], in_=bank[:, :], func=AF.Identity, bias=b.hyb[:, lh, 0, n:n + 1], scale=1.0), rd=[bk], wr=[("rb", s)])
                tr.dma(d[f"UT{n}"][:, bass.ds(tok, T)], b.rbuf[s][:, :], rd=[("rb", s)])
            linear_tile(b, tr, lambda kc: b.xn[:, kc, :], ["xn"], Wb, D, 3 * D, epih)
        tr.end_iter()


def attn_core(b, tr, nseg):
    nc = b.nc
    d = b.dram

    def loads(h, sg):
        p = h % 2
        for g3 in range(NG):
            nq = (g3 * 3 + 0) * NH + h
            nk_ = (g3 * 3 + 1) * NH + h
            nv = (g3 * 3 + 2) * NH + h
            tr.dma(b.qT[p][g3][:, :], d[f"QKV{nq}"][:, bass.ds(sg * SEG + PAD, SEG)], wr=[("qT", p, g3)])
            tr.dma(b.kT[p][g3][:, :], d[f"QKV{nk_}"][:, bass.ds(sg * SEG, SEG + 2 * PAD)], wr=[("kT", p, g3)])
            tr.dma(b.vT[p][g3][:, :], d[f"QKV{nv}"][:, bass.ds(sg * SEG, SEG + 2 * PAD)], wr=[("vT", p, g3)])

    with fori(b, nseg) as sg:
        tr.dma(b.smask[:, :], d["MASKTB"][bass.ds(sg, 1), :, :].rearrange("o p m -> (o p) m"), wr=["smask"])
        nacc = 0
        for h in range(NH):
            if h > 0:
                tr.end_iter()
                tr.dma(b.smask[:, :], d["MASKTB"][bass.ds(sg, 1), :, :].rearrange("o p m -> (o p) m"), wr=["smask"])
            loads(h, sg)
            p = h % 2
            anum, aden, obf = b.anum[p], b.aden[p], b.obf[p]
            ka, kd_, ko = ("anum", p), ("aden", p), ("obf", p)
            for g3 in range(NG):
                dd = DILS[g3]
                n = SEG // dd
                nkb = n // 128 + 1
                qT, kT, vT = b.qT[p][g3], b.kT[p][g3], b.vT[p][g3]
                kq, kk, kv = ("qT", p, g3), ("kT", p, g3), ("vT", p, g3)
                for r in range(dd):
                    for m in range(nkb):
                        qa = max(0, 128 * (m - 1))
                        qb_ = min(n, 128 * (m + 1))
                        nq = qb_ - qa
                        kcol = PAD + r + dd * (128 * m - 64)
                        ksl = slice(kcol, kcol + 127 * dd + 1, dd)
                        qcol = r + dd * qa
                        qsl = slice(qcol, qcol + (nq - 1) * dd + 1, dd)
                        if m == 0:
                            mview = b.smask[:, 0:128]
                            mk = ["smask"]
                        elif m == nkb - 1:
                            mview = b.smask[:, 128:256]
                            mk = ["smask"]
                        else:
                            mview = b.mask[:, 0:2, :].rearrange("p a b -> p (a b)")
                            mk = []
                        i2 = nacc % 2
                        nacc += 1
                        sT = b.ps_s[i2]
                        tr.mm(sT[:, 0:nq], [(kT[:, ksl], qT[:, qsl])], rd=[kk, kq], wr=[("ps_s", i2)])
                        tr.op("pe", lambda e: e.transpose(out=b.ps_vt[i2][:, 0:128], in_=vT[:, ksl], identity=b.ident_b[:, :]), rd=[kv], wr=[("ps_vt", i2)])
                        tr.op("act", lambda e: e.activation(out=b.pexp[i2][:, 0:nq], in_=sT[:, 0:nq], func=AF.Exp), rd=[("ps_s", i2)], wr=[("pexp", i2)])
                        tr.op("act", lambda e: e.activation(out=b.vblk[i2][:, :], in_=b.ps_vt[i2][:, 0:128], func=AF.Copy), rd=[("ps_vt", i2)], wr=[("vblk", i2)])
                        tr.op("pool", lambda e: e.tensor_tensor(out=b.pm[i2][:, 0:nq], in0=b.pexp[i2][:, 0:nq], in1=mview, op=ALU.mult), rd=[("pexp", i2)] + mk, wr=[("pm", i2)])
                        tr.mm(b.ps_num[i2][:, 0:nq], [(b.vblk[i2][:, :], b.pm[i2][:, 0:nq])], rd=[("vblk", i2), ("pm", i2)], wr=[("ps_num", i2)])
                        tr.mm(b.ps_den[i2][:, 0:nq], [(b.ones_b[:, :], b.pm[i2][:, 0:nq])], rd=[("pm", i2)], wr=[("ps_den", i2)])
                        asl = slice(qcol, qcol + (nq - 1) * dd + 1, dd)
                        pn, pd_ = b.ps_num[i2], b.ps_den[i2]
                        kn, kd = ("ps_num", i2), ("ps_den", i2)
                        if g3 == 0 and m == 0:
                            tr.op("dve", lambda e: e.tensor_copy(out=anum[:, asl], in_=pn[:, 0:nq]), rd=[kn], wr=[ka])
                            tr.op("dve", lambda e: e.tensor_copy(out=aden[:, asl], in_=pd_[:, 0:nq]), rd=[kd], wr=[kd_])
                        elif g3 == 0 and nq == 256:
                            a1 = slice(qcol, qcol + 128)
                            a2 = slice(qcol + 128, qcol + 256)
                            tr.op("dve", lambda e: e.tensor_tensor(out=anum[:, a1], in0=anum[:, a1], in1=pn[:, 0:128], op=ALU.add), rd=[kn, ka], wr=[ka])
                            tr.op("dve", lambda e: e.tensor_copy(out=anum[:, a2], in_=pn[:, 128:256]), rd=[kn], wr=[ka])
                            tr.op("dve", lambda e: e.tensor_tensor(out=aden[:, a1], in0=aden[:, a1], in1=pd_[:, 0:128], op=ALU.add), rd=[kd, kd_], wr=[kd_])
                            tr.op("dve", lambda e: e.tensor_copy(out=aden[:, a2], in_=pd_[:, 128:256]), rd=[kd], wr=[kd_])
                        else:
                            tr.op("dve", lambda e: e.tensor_tensor(out=anum[:, asl], in0=anum[:, asl], in1=pn[:, 0:nq], op=ALU.add), rd=[kn, ka], wr=[ka])
                            tr.op("dve", lambda e: e.tensor_tensor(out=aden[:, asl], in0=aden[:, asl], in1=pd_[:, 0:nq], op=ALU.add), rd=[kd, kd_], wr=[kd_])
            tr.op("dve", lambda e: e.reciprocal(out=aden[:, :], in_=aden[:, :]), rd=[kd_], wr=[kd_])
            tr.op("dve", lambda e: e.tensor_tensor(out=obf[:, :], in0=anum[:, :], in1=aden[:, :], op=ALU.mult), rd=[ka, kd_], wr=[ko])
            tr.dma(d[f"OT{h}"][:, bass.ds(sg * SEG, SEG)], obf[:, :], rd=[ko])
        tr.end_iter()


def hy_conv_gate(b, tr, seqs, lh):
    d = b.dram
    starts = {s0 for (s0, L) in seqs}
    ends = {s0 + L for (s0, L) in seqs}
    for cc in range(KC):
        for t0 in range(0, b.ntok, TB):
            for part in range(3):
                ub = b.ubuf[part]
                UT = d[f"UT{part * KC + cc}"]
                lo = t0 - 1 if t0 not in starts else t0
                hi = t0 + TB + 1 if (t0 + TB) not in ends else t0 + TB
                if t0 in starts:
                    tr.op("dve", lambda e: e.memset(ub[:, 0:1], 0.0), wr=[("ub", part)])
                if (t0 + TB) in ends:
                    tr.op("dve", lambda e: e.memset(ub[:, TB + 1:TB + 2], 0.0), wr=[("ub", part)])
                c0 = 1 - (t0 - lo)
                tr.dma(ub[:, c0:c0 + hi - lo], UT[:, lo:hi], wr=[("ub", part)])
                cw = b.hycw
                col = slice(cc + part * KC, cc + part * KC + 1)
                o = b.cbuf[part]
                tr.op("dve", lambda e: e.tensor_scalar(out=o[:, :], in0=ub[:, 1:TB + 1], scalar1=cw[:, lh, 1, col], scalar2=b.hyb[:, lh, 1, col],
                                                       op0=ALU.mult, op1=ALU.add), rd=[("ub", part)], wr=[("cb", part)])
                tr.op("dve", lambda e: e.scalar_tensor_tensor(out=o[:, :], in0=ub[:, 0:TB], scalar=cw[:, lh, 0, col], in1=o[:, :],
                                                              op0=ALU.mult, op1=ALU.add), rd=[("ub", part), ("cb", part)], wr=[("cb", part)])
                tr.op("dve", lambda e: e.scalar_tensor_tensor(out=o[:, :], in0=ub[:, 2:TB + 2], scalar=cw[:, lh, 2, col], in1=o[:, :],
                                                              op0=ALU.mult, op1=ALU.add), rd=[("ub", part), ("cb", part)], wr=[("cb", part)])
            tr.op("pool", lambda e: e.tensor_tensor(out=b.zbf[:, :], in0=b.cbuf[2][:, :], in1=b.cbuf[1][:, :], op=ALU.mult), rd=[("cb", 1), ("cb", 2)], wr=["zbf"])
            hf, cr = divmod(cc, KC // 2)
            tr.dma(d[f"ZT{hf}"][cr * 128:(cr + 1) * 128, t0:t0 + TB], b.zbf[:, :], rd=["zbf"])
            tr.dma(d["X0T"][cc * 128:(cc + 1) * 128, t0:t0 + TB], b.cbuf[0][:, :], rd=[("cb", 0)])
        tr.end_iter()


def hy_filters(b, tr, lh, Ls):
    nc = b.nc
    d = b.dram
    Q = "sp"
    tr.op("dve", lambda e: e.memset(b.fw4[:, :], 0.0), wr=["fw4"])
    tr.op("dve", lambda e: e.memset(b.zf[:, :], 0.0), wr=["zf"])
    tr.dma(b.fw4[0:64, :], d["hy_filt_w4"][lh], wr=["fw4"])
    tr.end_iter()
    for L in Ls:
        ZF = d[f"ZF{L}"]
        TP = d[f"TP{L}"]
        K2v = [d[f"K2_{L}_{hf}"].rearrange("(c p) m -> p c m", p=128) for hf in range(2)]
        for it in range(L // FT):
            for half in range(2):
                m0 = it * FT if half == 0 else it * FT + L
                tr.dma(b.zf[0:33, :], ZF[:, m0:m0 + FT], wr=["zf"], q=Q)
                tr.dma(b.tp[:, :], TP[:, m0:m0 + FT], wr=["tp"], q=Q)
                src = b.zf[:, :]
                srck = "zf"
                for layer in range(3):
                    wl = b.fw1[:, lh, :] if layer == 0 else b.fw23[:, lh, layer - 1, :]
                    ps = b.pbank[layer % 2]
                    pk = ("fps", layer % 2)
                    tr.mm(ps[:, :], [(wl, src)], rd=[srck], wr=[pk])
                    y = b.fy
                    tr.op("dve", lambda e: e.tensor_scalar(out=y[:, :], in0=ps[:, :], scalar1=b.fb[:, lh, layer:layer + 1], scalar2=b.ffr[:, lh:lh + 1],
                                                           op0=ALU.add, op1=ALU.mult), rd=[pk], wr=["fy"])
                    for rep in range(2):
                        tr.op("dve", lambda e: e.tensor_scalar(out=b.fm[:, :], in0=y[:, :], scalar1=float(np.pi), scalar2=-TWO_PI, op0=ALU.is_gt, op1=ALU.mult), rd=["fy"], wr=["fm"])
                        tr.op("dve", lambda e: e.tensor_tensor(out=y[:, :], in0=y[:, :], in1=b.fm[:, :], op=ALU.add), rd=["fm", "fy"], wr=["fy"])
                        tr.op("dve", lambda e: e.tensor_scalar(out=b.fm[:, :], in0=y[:, :], scalar1=-float(np.pi), scalar2=TWO_PI, op0=ALU.is_lt, op1=ALU.mult), rd=["fy"], wr=["fm"])
                        tr.op("dve", lambda e: e.tensor_tensor(out=y[:, :], in0=y[:, :], in1=b.fm[:, :], op=ALU.add), rd=["fm", "fy"], wr=["fy"])
                    hbuf = b.fh[layer % 2]
                    tr.op("act", lambda e: e.activation(out=hbuf[:, :], in_=y[:, :], func=AF.Sin), rd=["fy"], wr=[("fh", layer % 2)])
                    src = hbuf[:, :]
                    srck = ("fh", layer % 2)
                kst = b.kstage[half]
                for cc in range(KC):
                    ch0 = (0 if half == 1 else D) + cc * 128
                    s = cc % 2
                    ps = b.pbank[2 + s]
                    tr.mm(ps[:, :], [(b.fw4[:, ch0:ch0 + 128], src)], rd=[srck], wr=[("kps", s)])
                    tr.op("act", lambda e: e.activation(out=b.rbuf[s][:, :], in_=b.tp[:, :], func=AF.Exp, scale=b.negdelta[:, cc:cc + 1]), rd=["tp"], wr=[("dec", s)])
                    tr.op("dve", lambda e: e.tensor_tensor(out=kst[:, cc, :], in0=ps[:, :], in1=b.rbuf[s][:, :], op=ALU.mult), rd=[("kps", s), ("dec", s)], wr=[("kst", half)])
                for hf in range(2):
                    tr.dma(K2v[hf][:, :, m0:m0 + FT], kst[:, hf * 8:(hf + 1) * 8, :], rd=[("kst", half)], q=Q)
            tr.end_iter()


NCH = 2


def hy_longconv(b, tr, groups):
    nc = b.nc
    d = b.dram
    Q = "act"
    ntok = b.ntok
    with fori(b, D // NCH) as ci:
        for cl in range(NCH):
            for gi, (L, s0, ns) in enumerate(groups):
                nb = L // 128
                glen = L + 127
                gsrc = bass.AP(d[f"K2_{L}_{cl}"].tensor, ci * (2 * L) + 1, [[128, nb], [1, glen]])
                tr.dma(b.gbuf[cl][gi][0:nb, 0:glen], gsrc, wr=[("g", cl, gi)], q=Q)
                zsrc = bass.AP(d[f"ZT{cl}"].tensor, ci * ntok + s0, [[128, nb], [L, ns], [1, 128]])
                tr.dma(b.zc[cl][gi][0:nb, 0:ns, :], zsrc, wr=[("zc", cl, gi)], q=Q)
        for cl in range(NCH):
            for gi, (L, s0, ns) in enumerate(groups):
                nb = L // 128
                gb = b.gbuf[cl][gi]
                zc = b.zc[cl][gi]
                zr = b.zr[gi]
                J = b.j128 if nb == 128 else b.j16
                pz = b.pbank[1]
                pz3 = pz[0:nb, 0:ns * 128].rearrange("p (a t) -> p a t", a=ns)
                tr.mm(pz3, [(J[0:nb, 0:nb], zc[0:nb, 0:ns, :])], rd=[("zc", cl, gi)], wr=["pz"])
                tr.op("dve", lambda e: e.tensor_copy(out=zr[0:nb, 0:ns * 128], in_=pz[0:nb, 0:ns * 128]), rd=["pz"], wr=[("zr", gi)])
                ps = b.pbank[2 + (gi % 2)]
                pk = ("yps", gi % 2)
                us = [0] + [u for u in range(-127, 128) if u != 0]
                tr._deps("pe", [("g", cl, gi), ("zr", gi)], [pk])
                zr3 = zr[0:nb, 0:ns * 128].rearrange("p (a t) -> p a t", a=ns)
                ps3 = ps[0:nb, 0:ns * 128].rearrange("p (a t) -> p a t", a=ns)
                for idx, u in enumerate(us):
                    N = 128 - abs(u)
                    tlo = max(0, u)
                    slo = max(0, -u)
                    lhsT = gb[0:nb, u + 127:u + 127 + 128 * (nb - 1) + 1:128]
                    ins = nc.tensor.matmul(ps3[:, :, tlo:tlo + N], lhsT=lhsT, rhs=zr3[:, :, slo:slo + N],
                                           start=(idx == 0), stop=(idx == len(us) - 1), skip_group_check=True)
                cnt = tr.esem["pe"].inc(ins)
                tr._reg((tr.esem["pe"], cnt), [("g", cl, gi), ("zr", gi)], [pk])
                yst = b.yst[gi]
                tr.op("dve", lambda e: e.tensor_copy(out=yst[0:nb, 0:ns * 128], in_=ps[0:nb, 0:ns * 128]), rd=[pk], wr=[("yst", gi)])
                ydst = bass.AP(d[f"YC{cl}"].tensor, ci * ntok + s0, [[128, nb], [L, ns], [1, 128]])
                tr.dma(ydst, yst[0:nb, 0:ns * 128].rearrange("p (a t) -> p a t", a=ns), rd=[("yst", gi)], q=Q)
        tr.end_iter()


def hy_gate(b, tr, lh):
    d = b.dram
    for cc in range(KC):
        for t0 in range(0, b.ntok, TB):
            row = slice(cc * 128, (cc + 1) * 128)
            hf, cr = divmod(cc, KC // 2)
            rowh = slice(cr * 128, (cr + 1) * 128)
            tr.dma(b.cbuf[0][:, :], d[f"YC{hf}"][rowh, t0:t0 + TB], wr=[("cb", 0)])
            tr.dma(b.cbuf[1][:, :], d["X0T"][row, t0:t0 + TB], wr=[("cb", 1)])
            tr.dma(b.zbf[:, :], d[f"ZT{hf}"][rowh, t0:t0 + TB], wr=["zbf"])
            tr.op("dve", lambda e: e.scalar_tensor_tensor(out=b.cbuf[0][:, :], in0=b.zbf[:, :], scalar=b.hysk[:, lh, cc:cc + 1], in1=b.cbuf[0][:, :],
                                                          op0=ALU.mult, op1=ALU.add), rd=["zbf", ("cb", 0)], wr=[("cb", 0)])
            tr.op("dve", lambda e: e.tensor_tensor(out=b.ybf[:, :], in0=b.cbuf[0][:, :], in1=b.cbuf[1][:, :], op=ALU.mult), rd=[("cb", 0), ("cb", 1)], wr=["ybf"])
            tr.dma(d[f"OT{cc}"][:, t0:t0 + TB], b.ybf[:, :], rd=["ybf"])
        tr.end_iter()


def bufspec(name):
    f = lambda: ([128, T], F32)
    S = {
        "x_sb": ([128, KC, T], F32), "xn": ([128, KC, T], BF16), "h_sb": ([128, DFF // 128, T], BF16), "stage": 2 * [([128, T], BF16)],
        "wbuf": 3 * [([128, WBLK], BF16)], "sq": 2 * [f()], "rbuf": 2 * [f()], "rt": f(), "rstd": f(),
        "ropec": f(), "ropes": f(), "qg": 2 * [f()], "t1": f(), "t2": f(),
        "anum": 2 * [([128, SEG], F32)], "aden": 2 * [([128, SEG], F32)], "obf": 2 * [([128, SEG], BF16)], "smask": ([128, 256], BF16),
        "pexp": 2 * [([128, 256], F32)], "pm": 2 * [([128, 256], BF16)], "vblk": 2 * [([128, 128], BF16)],
        "ubuf": 3 * [([128, TB + 2], F32)], "cbuf": 3 * [([128, TB], F32)], "zbf": ([128, TB], BF16), "ybf": ([128, TB], BF16),
        "zf": ([128, FT], F32), "tp": ([128, FT], F32), "fy": ([128, FT], F32), "fm": ([128, FT], F32),
        "fh": 2 * [([128, FT], F32)], "fw4": ([128, 2 * D], F32), "kstage": 2 * [([128, KC, FT], BF16)],
        "zr": 2 * [([128, 256], BF16)], "yst": 2 * [([128, 256], F32)],
        "zpad": ([128, PAD], BF16), "cf": 2 * [([128, WBLK], F32)], "cbf": 2 * [([128, WBLK], BF16)],
    }
    return S[name]


class Phase:
    def __init__(self, b, names, extra=None):
        self.b = b
        self.names = names
        self.extra = extra

    def __enter__(self):
        self.st = ExitStack()
        b = self.b
        mk = lambda nm, sh, dt: self.st.enter_context(b.nc.sbuf_tensor(f"ph_{nm}_{b.nc.next_id()}", sh, dt))
        for n in self.names:
            spec = bufspec(n)
            if isinstance(spec, list):
                setattr(b, n, [mk(f"{n}{i}", sh, dt) for i, (sh, dt) in enumerate(spec)])
            else:
                setattr(b, n, mk(n, spec[0], spec[1]))
        if self.extra:
            self.extra(mk)
        return self

    def __exit__(self, *a):
        self.st.close()
        return False


TOK = ["x_sb", "xn", "h_sb", "wbuf", "sq", "rbuf", "rt", "rstd", "ropec", "ropes", "qg", "t1", "t2", "stage"]


def build_program(ntok, seqs, depth=4, stop=None):
    b = Builder(ntok)
    nc = b.nc
    NT2 = ntok + 2 * PAD
    nseg = ntok // SEG
    Ls = sorted({L for (_, L) in seqs}, reverse=True)
    groups = []
    for L in Ls:
        ss = sorted(s0 for (s0, l_) in seqs if l_ == L)
        assert all(ss[i] == ss[0] + i * L for i in range(len(ss))) and len(ss) <= 2
        groups.append((L, ss[0], len(ss)))
    xT = b.din("xT", [D, ntok])
    b.din("mix_norm", [4, D]); b.din("mlp_norm", [4, D])
    wqkv = b.din("attn_w_qkv", [2, D, QKVW]); b.din("attn_q_gain", [2, 3, HD]); b.din("attn_k_gain", [2, 3, HD])
    wao = b.din("attn_w_out", [2, D, D])
    hwin = b.din("hy_w_in", [2, D, 3 * D]); b.din("hy_b_in", [2, 3 * D]); b.din("hy_conv_w", [2, 3, 3 * D]); b.din("hy_conv_b", [2, 3 * D])
    b.din("hy_filt_w1", [2, 33, 64]); b.din("hy_filt_b1", [2, 64]); b.din("hy_filt_w2", [2, 64, 64]); b.din("hy_filt_b2", [2, 64])
    b.din("hy_filt_w3", [2, 64, 64]); b.din("hy_filt_b3", [2, 64]); b.din("hy_filt_freq", [2, 64]); b.din("hy_filt_w4", [2, 64, 2 * D])
    b.din("hy_skip", [2, D]); hwo = b.din("hy_w_out", [2, D, D]); b.din("hy_b_out", [2, D])
    w1 = b.din("mlp_w1", [4, D, DFF]); w2 = b.din("mlp_w2", [4, DFF, D])
    b.din("ropeC", [128, ntok]); b.din("ropeS", [128, ntok]); b.din("masks", [128, 4, 128]); b.din("perm", [128, 128])
    b.din("negdelta", [128, KC]); b.din("MASKT", [nseg, 128, 2, 128], BF16 if False else F32); b.din("J128", [128, 128]); b.din("J16", [16, 16])
    for L in Ls:
        b.din(f"ZF{L}", [33, 2 * L]); b.din(f"TP{L}", [128, 2 * L])
        for hf in range(2):
            b.dint(f"K2_{L}_{hf}", [D // 2, 2 * L], BF16)
    b.XT = b.dout("yT", [D, ntok])
    d = b.dram

    def blocked(name, K, N, nl):
        nk, cb, nblk = wblocks(K, N)
        return [b.dint(f"{name}{l}", [nblk, 128, WBLK], BF16) for l in range(nl)]
    wqkvb = blocked("wqkvb", D, QKVW, 2); waob = blocked("waob", D, D, 2)
    hwinb = blocked("hwinb", D, 3 * D, 2); hwob = blocked("hwob", D, D, 2)
    w1b = blocked("w1b", D, DFF, 4); w2b = blocked("w2b", DFF, D, 4)
    for i in range(QKVW // 128):
        b.dint(f"QKV{i}", [128, NT2], BF16)
    for i in range(3 * KC):
        b.dint(f"UT{i}", [128, ntok], F32)
    for i in range(KC):
        b.dint(f"OT{i}", [128, ntok], BF16)
    for hf in range(2):
        b.dint(f"ZT{hf}", [D // 2, ntok], BF16); b.dint(f"YC{hf}", [D // 2, ntok], F32)
    b.dint("X0T", [D, ntok], F32)
    b.dint("MASKTB", [nseg, 128, 256], BF16)
    with b.stack:
        sb = b.sb
        b.ones_f = sb("ones_f", [128, 128], F32); b.ones_b = sb("ones_b", [128, 128], BF16)
        b.perm_f = sb("perm_f", [128, 128], F32); b.ident_b = sb("ident_b", [128, 128], BF16)
        b.j128 = sb("j128", [128, 128], BF16); b.j16 = sb("j16", [16, 16], BF16)
        b.eps_c = sb("eps_c", [128, 1], F32)
        b.gmix = sb("gmix", [128, 4, KC], F32); b.gmlp = sb("gmlp", [128, 4, KC], F32)
        b.qkgain = sb("qkgain", [128, 2, 6], F32); b.mask = sb("mask", [128, 4, 128], BF16)
        b.hyb = sb("hyb", [128, 2, 2, 48], F32); b.hycw = sb("hycw", [128, 2, 3, 48], F32)
        b.hysk = sb("hysk", [128, 2, KC], F32); b.hybo = sb("hybo", [128, 2, KC], F32); b.negdelta = sb("negdelta", [128, KC], F32)
        b.fw1 = sb("fw1", [128, 2, 128], F32); b.fw23 = sb("fw23", [128, 2, 2, 128], F32); b.fb = sb("fb", [128, 2, 3], F32); b.ffr = sb("ffr", [128, 2], F32)
        b.pbank = [b.ps(f"pb{i}", [128, 512]) for i in range(4)]
        b.ps_ss = b.ps("ps_ss", [128, 512]); b.ps_rq = b.ps("ps_rq", [128, 512])
        b.ps_vt = [b.ps(f"ps_vt{i}", [128, 1024], BF16) for i in range(2)]
        b.ps_s = b.pbank[0:2]; b.ps_num = b.pbank[2:4]; b.ps_den = [b.ps_ss, b.ps_rq]
        b.s_cast = b.sem("s_cast")
        tr = Tracker(b)
        with Phase(b, ["zpad"], None) as ph:
            maskf = ph.st.enter_context(nc.sbuf_tensor("maskf", [128, 4, 128], F32))
            jf = ph.st.enter_context(nc.sbuf_tensor("jf", [128, 128], F32))
            jf16 = ph.st.enter_context(nc.sbuf_tensor("jf16", [16, 16], F32))
            mt = ph.st.enter_context(nc.sbuf_tensor("mt", [128, 256], F32))
            mtb = ph.st.enter_context(nc.sbuf_tensor("mtb", [128, 256], BF16))
            nc.vector.memset(b.ones_f[:, :], 1.0); nc.vector.memset(b.ones_b[:, :], 1.0); nc.vector.memset(b.eps_c[:, :], EPS)
            nc.vector.memset(b.zpad[:, :], 0.0)
            nc.vector.memset(b.fw1[:, :, :], 0.0); nc.vector.memset(b.fw23[:, :, :, :], 0.0)
            nc.vector.memset(b.fb[:, :, :], 0.0); nc.vector.memset(b.ffr[:, :], 0.0)
            nc.all_engine_barrier()
            ns = dict(allow_slow_non_contiguous=True)
            tr.dma(b.gmix[:, :, :], d["mix_norm"].rearrange("l (c p) -> p l c", p=128), wr=["c0"], **ns)
            tr.dma(b.gmlp[:, :, :], d["mlp_norm"].rearrange("l (c p) -> p l c", p=128), wr=["c1"], **ns)
            for la in range(2):
                tr.dma(b.qkgain[:, la, 0:6:2], d["attn_q_gain"][la].rearrange("g p -> p g"), wr=["c2"], **ns)
                tr.dma(b.qkgain[:, la, 1:6:2], d["attn_k_gain"][la].rearrange("g p -> p g"), wr=["c2"], **ns)
            tr.dma(maskf[:, :, :], d["masks"][:, :, :], wr=["maskf"])
            tr.dma(jf[:, :], d["J128"][:, :], wr=["jf"])
            tr.dma(jf16[:, :], d["J16"][:, :], wr=["jf16"])
            tr.dma(b.perm_f[:, :], d["perm"][:, :], wr=["c3"])
            tr.dma(b.negdelta[:, :], d["negdelta"][:, :], wr=["c4"])
            for lh in range(2):
                tr.dma(b.hyb[:, lh, 0, :], d["hy_b_in"][lh:lh + 1, :].rearrange("o (c p) -> p (o c)", p=128), wr=["c5"], **ns)
                tr.dma(b.hyb[:, lh, 1, :], d["hy_conv_b"][lh:lh + 1, :].rearrange("o (c p) -> p (o c)", p=128), wr=["c5"], **ns)
                for k_ in range(3):
                    tr.dma(b.hycw[:, lh, k_, :], d["hy_conv_w"][lh, k_:k_ + 1, :].rearrange("o (c p) -> p (o c)", p=128), wr=["c6"], **ns)
                tr.dma(b.fw1[0:33, lh, 0:64], d["hy_filt_w1"][lh], wr=["c7"])
                tr.dma(b.fw23[0:64, lh, 0, 0:64], d["hy_filt_w2"][lh], wr=["c7"])
                tr.dma(b.fw23[0:64, lh, 1, 0:64], d["hy_filt_w3"][lh], wr=["c7"])
                for k_, nm in enumerate(["hy_filt_b1", "hy_filt_b2", "hy_filt_b3"]):
                    tr.dma(b.fb[0:64, lh, k_:k_ + 1], d[nm][lh:lh + 1, :].rearrange("o p -> p o"), wr=["c8"], **ns)
                tr.dma(b.ffr[0:64, lh:lh + 1], d["hy_filt_freq"][lh:lh + 1, :].rearrange("o p -> p o"), wr=["c8"], **ns)
            tr.dma(b.hysk[:, :, :], d["hy_skip"].rearrange("l (c p) -> p l c", p=128), wr=["c9"], **ns)
            tr.dma(b.hybo[:, :, :], d["hy_b_out"].rearrange("l (c p) -> p l c", p=128), wr=["c9"], **ns)
            tr.op("dve", lambda e: e.tensor_copy(out=b.mask[:, :, :], in_=maskf[:, :, :]), rd=["maskf"], wr=["mask"])
            tr.op("dve", lambda e: e.tensor_tensor(out=b.ident_b[:, :], in0=maskf[:, 0, :], in1=maskf[:, 1, :], op=ALU.mult), rd=["maskf"], wr=["ident"])
            tr.op("dve", lambda e: e.tensor_copy(out=b.j128[:, :], in_=jf[:, :]), rd=["jf"], wr=["j128"])
            tr.op("dve", lambda e: e.tensor_copy(out=b.j16[:, :], in_=jf16[:, :]), rd=["jf16"], wr=["j16"])
            tr.op("dve", lambda e: e.tensor_scalar(out=b.qkgain[:, :, 0:6:2], in0=b.qkgain[:, :, 0:6:2], scalar1=float(HD ** -0.5), scalar2=None, op0=ALU.mult), rd=["c2"], wr=["c2"])
            for sgi in range(nseg):
                tr.dma(mt[:, :], d["MASKT"][sgi].rearrange("p a b -> p (a b)"), wr=["mt"])
                tr.op("dve", lambda e: e.tensor_copy(out=mtb[:, :], in_=mt[:, :]), rd=["mt"], wr=["mtb"])
                tr.dma(d["MASKTB"][sgi], mtb[:, :], rd=["mtb"])
            for g3 in range(NG):
                for a in (1, 2):
                    for h in range(NH):
                        tv = d[f"QKV{(g3 * 3 + a) * NH + h}"]
                        tr.dma(tv[:, 0:PAD], b.zpad[:, :], rd=["zp"])
                        tr.dma(tv[:, PAD + ntok:NT2], b.zpad[:, :], rd=["zp"])
            tr.end_iter()
        b.cast_ctr = 0
        with Phase(b, ["cf", "cbf"]):
            for l in range(2):
                cast_blocked(b, tr, wqkvb[l], wqkv[l], D, QKVW); cast_blocked(b, tr, waob[l], wao[l], D, D)
                cast_blocked(b, tr, hwinb[l], hwin[l], D, 3 * D); cast_blocked(b, tr, hwob[l], hwo[l], D, D)
            for l in range(4):
                cast_blocked(b, tr, w1b[l], w1[l], D, DFF); cast_blocked(b, tr, w2b[l], w2[l], DFF, D)
            tr.end_iter()
        xsrc = xT
        pre = None
        mlp = None
        for layer in range(depth + 1):
            j = layer // 2
            if layer == depth:
                post = None
            elif layer % 2 == 0:
                post = ("qkv", wqkvb[j], b.gmix[:, layer, :], j)
            else:
                post = ("hyin", hwinb[j], b.gmix[:, layer, :], j)
            with Phase(b, TOK):
                token_loop(b, tr, xsrc, pre, mlp, post)
            xsrc = b.XT
            if layer == depth or stop == f"tok{layer}":
                break
            if layer % 2 == 0:
                def extra_a(mk):
                    b.qT = [[mk(f"qT{p}{g}", [128, SEG], BF16) for g in range(NG)] for p in range(2)]
                    b.kT = [[mk(f"kT{p}{g}", [128, SEG + 2 * PAD], BF16) for g in range(NG)] for p in range(2)]
                    b.vT = [[mk(f"vT{p}{g}", [128, SEG + 2 * PAD], BF16) for g in range(NG)] for p in range(2)]
                with Phase(b, ["anum", "aden", "obf", "smask", "pexp", "pm", "vblk"], extra_a):
                    attn_core(b, tr, nseg)
                if stop == "attn0":
                    break
                pre = (waob[j], None)
            else:
                with Phase(b, ["ubuf", "cbuf", "zbf"]):
                    hy_conv_gate(b, tr, seqs, j)
                if stop == "hcg":
                    break
                with Phase(b, ["zf", "tp", "fy", "fm", "fh", "rbuf", "kstage", "fw4"]):
                    hy_filters(b, tr, j, Ls)
                if stop == "hfil":
                    break

                def extra(mk):
                    b.gbuf = [[mk(f"g{cl}_{gi}", [L // 128, L + 128], BF16) for gi, (L, _, _) in enumerate(groups)] for cl in range(NCH)]
                    b.zc = [[mk(f"zc{cl}_{gi}", [L // 128, 2, 128], BF16) for gi, (L, _, _) in enumerate(groups)] for cl in range(NCH)]
                with Phase(b, ["zr", "yst"], extra):
                    hy_longconv(b, tr, groups)
                if stop == "hlc":
                    break
                with Phase(b, ["cbuf", "zbf", "ybf"]):
                    hy_gate(b, tr, j)
                pre = (hwob[j], b.hybo[:, j, :])
            mlp = (w1b[layer], w2b[layer], b.gmlp[:, layer, :])
    return nc


def const_tables(seqs, ntok):
    half = HD // 2
    inv = (10000.0 ** (-np.arange(0, HD, 2, dtype=np.float32) / HD)).astype(np.float32)
    ropeC = np.zeros((128, ntok), np.float32); ropeS = np.zeros((128, ntok), np.float32)
    nseg = ntok // SEG
    k = np.arange(128)[:, None]; j = np.arange(128)[None, :]
    mB, mA = (k <= j), (k >= j)
    mA0, mBl = mA & (k >= 64), mB & (k < 64)
    masks = np.stack([mB, mA, mA0, mBl], axis=1).astype(np.float32)
    maskt = np.zeros((nseg, 128, 2, 128), np.float32)
    for sg in range(nseg):
        maskt[sg, :, 0, :] = mA
        maskt[sg, :, 1, :] = mB
    for (s0, L) in seqs:
        ang = np.arange(L, dtype=np.float32)[:, None] * inv[None, :]
        c = np.cos(ang).T.astype(np.float32); s = np.sin(ang).T.astype(np.float32)
        ropeC[:half, s0:s0 + L] = c; ropeC[half:, s0:s0 + L] = c
        ropeS[:half, s0:s0 + L] = -s; ropeS[half:, s0:s0 + L] = s
        maskt[s0 // SEG, :, 0, :] = mA0
        maskt[(s0 + L) // SEG - 1, :, 1, :] = mBl
    perm = np.zeros((128, 128), np.float32)
    perm[np.arange(128), (np.arange(128) + 64) % 128] = 1.0
    deltas = np.abs(np.linspace(np.log(1e-2) / 1.5, np.log(1e-2) / 0.3, D, dtype=np.float32))
    negdelta = np.ascontiguousarray((-deltas).reshape(KC, 128).T).astype(np.float32)
    out = {"ropeC": ropeC, "ropeS": ropeS, "masks": np.ascontiguousarray(masks), "perm": perm, "negdelta": negdelta,
           "MASKT": maskt, "J128": np.ascontiguousarray(np.eye(128, dtype=np.float32)[::-1]),
           "J16": np.ascontiguousarray(np.eye(16, dtype=np.float32)[::-1])}
    for L in sorted({L for (_, L) in seqs}):
        m = np.arange(2 * L)
        pos = np.where(m >= L, m - L, L - m).astype(np.float32)
        t = pos / np.float32(L - 1)
        bands = np.linspace(1e-4, 15, 16, dtype=np.float32)
        ang = (np.float32(2.0 * np.pi / L) * pos[:, None] * bands[None, :]).astype(np.float32)
        z = np.concatenate([t[:, None], np.cos(ang), -np.sin(ang)], axis=-1).astype(np.float32)
        out[f"ZF{L}"] = np.ascontiguousarray(z.T)
        out[f"TP{L}"] = np.ascontiguousarray(np.broadcast_to(t[None, :], (128, 2 * L))).astype(np.float32)
    return out


WNAMES = ["mix_norm", "mlp_norm", "attn_w_qkv", "attn_q_gain", "attn_k_gain", "attn_w_out", "hy_w_in", "hy_b_in", "hy_conv_w",
          "hy_conv_b", "hy_filt_w1", "hy_filt_b1", "hy_filt_w2", "hy_filt_b2", "hy_filt_w3", "hy_filt_b3", "hy_filt_freq",
          "hy_filt_w4", "hy_skip", "hy_w_out", "hy_b_out", "mlp_w1", "mlp_w2"]


def kernel(x_prompt, x_sample, **w):
    seqs = [(0, 16384), (16384, 2048)]
    ntok = 18432
    nc = build_program(ntok, seqs)
    tabs = const_tables(seqs, ntok)
    weights = {k: np.ascontiguousarray(np.asarray(w[k], dtype=np.float32)) for k in WNAMES}
    xp = np.asarray(x_prompt); xs = np.asarray(x_sample)
    in_maps = []
    for c in range(8):
        xT = np.zeros((D, ntok), np.float32)
        if c < 2:
            xT[:, 0:16384] = xs[c].T
        if c < 4:
            xT[:, 16384:18432] = xp[c].T
        m = {"xT": xT}
        m.update(weights); m.update(tabs)
        in_maps.append(m)
    res = run_bass_kernel_spmd(nc, in_maps, core_ids=list(range(8)))
    yp = np.zeros(xp.shape, np.float32); ys = np.zeros(xs.shape, np.float32)
    for c in range(4):
        y = res.results[c]["yT"]
        if c < 2:
            ys[c] = y[:, 0:16384].T
        yp[c] = y[:, 16384:18432].T
    return (yp, ys)
```
